# Optimizing a Trainium2 kernel written in Bass

```python
import jax, jax.numpy as jnp
from jax import lax
import numpy as np

D_MODEL = 2048
BATCH = 2
SEQ = 16384
DEPTH = 2

W_A = D_MODEL // 2
W_B = D_MODEL // 2
W_C = D_MODEL // 2
A_GROUPS = 8
B_GROUPS = 8
CONV_A = 3
CONV_B = 31
POOL_WINDOWS = (2, 4, 8, 16)
N_POOL = len(POOL_WINDOWS)
C_GROUP = W_C // N_POOL
N_BRANCH = 3
RMS_EPS = 1e-6
LN_EPS = 1e-5
SPLIT_SIZES = (W_A, W_A, W_A, W_A, W_B, W_B, W_B, W_C, W_C, N_BRANCH * D_MODEL)
IN_COLS = sum(SPLIT_SIZES)
SPLIT_POINTS = tuple(int(v) for v in np.cumsum(SPLIT_SIZES)[:-1])

kernel_name = "hybrid_gated_conv_pool_trunk"


def rmsnorm(x, g):
    x32 = x.astype(jnp.float32)
    r = x32 * lax.rsqrt(jnp.mean(x32 * x32, axis=-1, keepdims=True) + RMS_EPS)
    return (r * g.astype(jnp.float32)).astype(x.dtype)


def layernorm(x, g, b):
    x32 = x.astype(jnp.float32)
    mu = jnp.mean(x32, axis=-1, keepdims=True)
    var = jnp.mean(jnp.square(x32 - mu), axis=-1, keepdims=True)
    y = (x32 - mu) * lax.rsqrt(var + LN_EPS)
    return (y * g.astype(jnp.float32) + b.astype(jnp.float32)).astype(x.dtype)


def causal_dwconv(u, w):
    k = w.shape[0]
    return lax.conv_general_dilated(
        u, w[:, None, :].astype(u.dtype), window_strides=(1,),
        padding=((k - 1, 0),), dimension_numbers=("NWC", "WIO", "NWC"),
        feature_group_count=u.shape[-1])


def short_conv_mixer(a_u, a_b, a_c, a_z, conv_w, w_br):
    y = a_b * causal_dwconv(a_c * a_u, conv_w)
    return (jax.nn.silu(a_z) * y) @ w_br


def conformer_conv_mixer(b_v, b_g, b_z, conv_w, conv_b, ln_g, ln_b, w_br):
    v = b_v * jax.nn.sigmoid(b_g)
    v = causal_dwconv(v, conv_w) + conv_b
    v = jax.nn.silu(layernorm(v, ln_g, ln_b))
    return (jax.nn.silu(b_z) * v) @ w_br


def pool_mixer(c_u, c_z, group_w, scale, w_br):
    bsz, seq, _ = c_u.shape
    cs = lax.cumsum(c_u.astype(jnp.float32), axis=1)
    p = jnp.pad(cs, ((0, 0), (1, 0), (0, 0)))
    pos = jnp.arange(seq, dtype=jnp.float32)[:, None] + 1.0
    groups = []
    for i, w in enumerate(POOL_WINDOWS):
        pg = p[..., i * C_GROUP:(i + 1) * C_GROUP]
        upper = pg[:, 1:]
        lower = jnp.pad(pg, ((0, 0), (w - 1, 0), (0, 0)))[:, :seq]
        cnt = jnp.minimum(pos, float(w))
        groups.append((upper - lower) / cnt)
    pooled = jnp.stack(groups, axis=2).astype(c_u.dtype)
    u = c_u.reshape(bsz, seq, N_POOL, C_GROUP)
    y = jnp.einsum("bsgc,gcd->bsgd", pooled - u, group_w.astype(c_u.dtype))
    y = y.reshape(bsz, seq, W_C) * scale
    return (jax.nn.silu(c_z) * y) @ w_br


def setup_inputs(seed: int = 0) -> dict:
    key = jax.random.key(seed)
    ks = jax.random.split(key, 20)
    f32 = jnp.float32
    n = lambda k, shape, s: jax.random.normal(k, shape, f32) * s
    return {
        "x": n(ks[0], (BATCH, SEQ, D_MODEL), 1.0),
        "norm_g": 1.0 + n(ks[1], (DEPTH, D_MODEL), 0.02),
        "w_in": n(ks[2], (DEPTH, D_MODEL, IN_COLS), D_MODEL ** -0.5),
        "a_conv_w": n(ks[3], (DEPTH, CONV_A, W_A), CONV_A ** -0.5),
        "b_conv_w": n(ks[4], (DEPTH, CONV_B, W_B), CONV_B ** -0.5),
        "b_conv_b": n(ks[5], (DEPTH, W_B), 0.02),
        "b_ln_g": 1.0 + n(ks[6], (DEPTH, W_B), 0.02),
        "b_ln_b": n(ks[7], (DEPTH, W_B), 0.02),
        "c_group_w": n(ks[8], (DEPTH, N_POOL, C_GROUP, C_GROUP), C_GROUP ** -0.5),
        "c_scale": 1.0 + n(ks[9], (DEPTH, W_C), 0.02),
        "w_br_a": n(ks[10], (DEPTH, W_A, D_MODEL), W_A ** -0.5),
        "w_br_b": n(ks[11], (DEPTH, W_B, D_MODEL), W_B ** -0.5),
        "w_br_c": n(ks[12], (DEPTH, W_C, D_MODEL), W_C ** -0.5),
        "w_o": n(ks[13], (DEPTH, D_MODEL, D_MODEL), D_MODEL ** -0.5),
        "final_g": 1.0 + n(ks[14], (D_MODEL,), 0.02),
    }


def reference(x, norm_g, w_in, a_conv_w, b_conv_w, b_conv_b, b_ln_g, b_ln_b,
              c_group_w, c_scale, w_br_a, w_br_b, w_br_c, w_o, final_g):
    bsz, seq, _ = x.shape
    for l in range(DEPTH):
        h = rmsnorm(x, norm_g[l])
        proj = h @ w_in[l]
        (a_u, a_b, a_c, a_z, b_v, b_g, b_z, c_u, c_z, gates) = jnp.split(
            proj, SPLIT_POINTS, axis=-1)
        y_a = short_conv_mixer(a_u, a_b, a_c, a_z, a_conv_w[l], w_br_a[l])
        y_b = conformer_conv_mixer(b_v, b_g, b_z, b_conv_w[l], b_conv_b[l],
                                   b_ln_g[l], b_ln_b[l], w_br_b[l])
        y_c = pool_mixer(c_u, c_z, c_group_w[l], c_scale[l], w_br_c[l])
        g = jax.nn.sigmoid(gates.reshape(bsz, seq, N_BRANCH, D_MODEL))
        merged = g[:, :, 0] * y_a + g[:, :, 1] * y_b + g[:, :, 2] * y_c
        x = x + merged @ w_o[l]
    return rmsnorm(x, final_g)
```

```python
import numpy as np
import concourse.bass as bass
import concourse.mybir as mybir
from concourse.bass_utils import run_bass_kernel_spmd
from contextlib import ExitStack

F32 = mybir.dt.float32
BF16 = mybir.dt.bfloat16
AF = mybir.ActivationFunctionType
ALU = mybir.AluOpType

D = 2048
NCH = 16
L = 2
IN_COLS = 15360
RMS_EPS = 1e-6
LN_EPS = 1e-5
HALO = 64
TILE = 512
NCORES = 8
COL = dict(au=0, ab=1024, ac=2048, az=3072, bv=4096, bg=5120, bz=6144, cu=7168, cz=8192,
           g0=9216, g1=11264, g2=13312)
CH = 56
NSLOT = 3
HOIST = (1, 2, 3)
NGP = 9
NBLK = 2576
NCHUNK = NBLK // CH
TMPW = 544
NTMP = 12
NBF = 10
DRAIN_K = 5
RELAX_DVE_RAW = False
BZ_AHEAD = 3

P_NG, P_FG, P_AW, P_BW, P_BB, P_LG, P_LB, P_CS, P_IC, P_ID, P_N = 0, 32, 48, 96, 592, 608, 624, 640, 656, 720, 848


def layer_jobs():
    jobs = []

    def win(kind, j):
        return (kind, j, ('w_in', COL[kind] // 128 + j, 16))

    def A(j):
        return [win('au', j), win('ac', j), win('ab', j), win('az', j)]

    def B1(j):
        return [win('bg', j), win('bv', j)]

    def C(g):
        return [win('cu', 2 * g), win('cu', 2 * g + 1), win('cz', 2 * g), win('cz', 2 * g + 1)]

    def G(g):
        return [('gy', 2 * g + m, ('gw%d' % g, m, 2)) for m in (0, 1)]

    hoisted = [win(g, i) for i in HOIST for g in ('g0', 'g1', 'g2')]
    hq = list(hoisted)
    for j in range(8):
        jobs += A(j) + B1(j)
        if hq:
            jobs.append(hq.pop(0))
        if j % 2 == 1:
            jobs += C(j // 2)
            if hq and len(hq) > 7 - j:
                jobs.append(hq.pop(0))
        if j >= 1:
            jobs.append(('st', j - 1, None))
        if j % 2 == 0 and j >= 2:
            jobs += G(j // 2 - 1)
    jobs += [win('bz', 0), win('bz', 1)] + G(3) + [win('g0', 0), win('g1', 0), win('g2', 0)] + [('st', 7, None)]
    for j in range(2, 8):
        jobs.append(win('bz', j))
    for i in range(16):
        if i > 0 and i not in HOIST:
            jobs += [win('g0', i), win('g1', i), win('g2', i)]
        jobs += [('ya', i, ('w_br_a', i, 8)), ('yb', i, ('w_br_b', i, 8)), ('yc', i, ('w_br_c', i, 8))]
    for i in range(16):
        jobs.append(('o', i, ('w_o', i, 16)))
    pos = {}
    p = 0
    for (_, _, src) in jobs:
        if src is not None:
            pos[(src[0], src[1])] = p
            p += src[2]
    assert p == NBLK, p
    return jobs, pos


class Sem:
    def __init__(self, h):
        self.h = h
        self.val = 0


class Eng:
    def __init__(self, name, sem):
        self.name = name
        self.sem = sem
        self.prog = []
        self.waited = {}


class Res:
    __slots__ = ('w', 'weng', 'rd')

    def __init__(self):
        self.w = None
        self.weng = None
        self.rd = {}


class T:
    def __init__(self, ap):
        self.ap = ap
        self.res = Res()
        self.hres = Res()


class Pool_:
    def __init__(self, tiles):
        self.free_list = list(tiles)

    def get(self):
        assert self.free_list, "scratch pool exhausted"
        return self.free_list.pop(0)

    def free(self, *ts):
        for t in ts:
            assert t not in self.free_list
            self.free_list.append(t)


def _need(E, reads, writes):
    need = {}

    def add(tok):
        s, v = tok
        if need.get(s, 0) < v:
            need[s] = v
    for r in reads:
        if r.w is not None:
            if RELAX_DVE_RAW and E.name == 'DVE' and r.weng == 'DVE' and r.w[1] <= E.sem.val - 1:
                continue
            add(r.w)
    for w in writes:
        if w.w is not None and w.weng != E.name:
            add(w.w)
        for en, tok in w.rd.items():
            if en != E.name:
                add(tok)
    for s, v in need.items():
        if E.waited.get(s, 0) < v:
            E.waited[s] = v
            E.prog.append(('w', s, v))


def op(E, fn, reads=(), writes=()):
    _need(E, reads, writes)
    E.sem.val += 1
    tok = (E.sem, E.sem.val)
    E.prog.append(('i', fn))
    for r in reads:
        r.rd[E.name] = tok
    for w in writes:
        w.w = tok
        w.weng = E.name
        w.rd = {}


def dma(Q, out, in_, sem, reads=(), writes=()):
    _need(Q, reads, writes)
    sem.val += 16
    tok = (sem, sem.val)
    Q.prog.append(('d', out, in_, sem))
    for r in reads:
        r.rd[('dma', id(sem))] = tok
    for w in writes:
        w.w = tok
        w.weng = 'dma'
        w.rd = {}


def build_program(tok_per_core):
    ntiles = tok_per_core // TILE
    tiles = [(0, HALO)] + [(HALO + i * TILE, TILE) for i in range(ntiles)]
    jobs, jobpos = layer_jobs()

    nc = bass.Bass("TRN2", target_bir_lowering=False)
    xh = nc.dram_tensor("xh", [HALO + tok_per_core, D], F32, kind="ExternalInput").ap()
    par_d = nc.dram_tensor("par", [128, P_N], F32, kind="ExternalInput").ap()
    w_in_d = nc.dram_tensor("w_in", [L, D, IN_COLS], F32, kind="ExternalInput").ap()
    w_br_d = {b: nc.dram_tensor("w_br_" + b, [L, 1024, D], F32, kind="ExternalInput").ap() for b in 'abc'}
    w_o_d = nc.dram_tensor("w_o", [L, D, D], F32, kind="ExternalInput").ap()
    gw_d = nc.dram_tensor("c_group_w", [L, 4, 256, 256], F32, kind="ExternalInput").ap()
    out_d = nc.dram_tensor("out", [tok_per_core, D], F32, kind="ExternalOutput").ap()
    wsc = nc.dram_tensor("wsc", [L, 128, NBLK * 128], BF16, kind="Internal").ap()

    es = ExitStack()
    with es:
        AW = 53100
        arena = es.enter_context(nc.sbuf_tensor("arena", [128, AW], F32))
        ps = es.enter_context(nc.psum_tensor("ps", [128, 8, 512], F32))
        _off = [0]

        def alloc(words):
            o = _off[0]
            _off[0] += words
            assert _off[0] <= AW, _off[0]
            return o

        def f32v(o, words):
            return arena[:, o:o + words]

        def bfv(o, words):
            return arena[:, o:o + words].bitcast(BF16)

        mksem = lambda name: Sem(es.enter_context(nc.semaphore(name)))
        PE = Eng('PE', mksem('s_pe'))
        ACT = Eng('ACT', mksem('s_act'))
        DVE = Eng('DVE', mksem('s_dve'))
        POOL = Eng('POOL', mksem('s_pool'))
        SP = Eng('SP', mksem('s_sp'))
        engines = [PE, ACT, DVE, POOL, SP]
        all_sems = [e.sem for e in engines]

        def dsem(name):
            s = mksem(name)
            all_sems.append(s)
            return s

        o_par = alloc(P_N)
        par = f32v(o_par, P_N)
        parR = Res()
        par_sem = dsem('s_par')
        ident = par[:, P_ID:P_ID + 128]
        o_st = alloc(848)
        st_all = f32v(o_st, 848)
        stA = lambda l, j: st_all[:, (l * 8 + j) * 8:(l * 8 + j) * 8 + 8]
        stB = lambda l, j: st_all[:, 128 + (l * 8 + j) * 30:128 + (l * 8 + j) * 30 + 30]
        stC = lambda l, j: st_all[:, 608 + (l * 8 + j) * 15:608 + (l * 8 + j) * 15 + 15]
        stAR = [[Res() for _ in range(8)] for _ in range(L)]
        stBR = [[Res() for _ in range(8)] for _ in range(L)]
        stCR = [[Res() for _ in range(8)] for _ in range(L)]
        o_ones = alloc(128)
        onesD = bfv(o_ones, 64)
        onesC = bfv(o_ones + 64, 64)
        o_eps = alloc(2)
        epsR_ap = f32v(o_eps, 1)
        epsL_ap = f32v(o_eps + 1, 1)
        constR = Res()
        o_main = _off[0]

        o_x = alloc(16 * 512)
        xT = f32v(o_x, 16 * 512).rearrange("p (c t) -> p c t", t=512)
        xTR = [Res() for _ in range(16)]
        o_h = alloc(16 * 256)
        hT = bfv(o_h, 16 * 256).rearrange("p (c t) -> p c t", t=512)
        hTR = [Res() for _ in range(16)]
        o_y = alloc(24 * 256)
        yin = bfv(o_y, 24 * 256).rearrange("p (c t) -> p c t", t=512)
        yinR = [Res() for _ in range(24)]
        oT = f32v(o_h, 16 * 512).rearrange("p (c t) -> p c t", t=512)

        def oTR(c):
            return [hTR[2 * c], hTR[2 * c + 1]] if c < 8 else [yinR[2 * (c - 8)], yinR[2 * (c - 8) + 1]]
        o_big = alloc(16 * 256)
        vc = f32v(o_big, 8 * 512).rearrange("p (c t) -> p c t", t=512)
        mg = bfv(o_big, 16 * 256).rearrange("p (c t) -> p c t", t=512)
        bigR = [Res() for _ in range(16)]
        o_ring = alloc(NSLOT * CH * 64)
        ring = [bfv(o_ring + s * CH * 64, CH * 64) for s in range(NSLOT)]
        ringR = [Res() for _ in range(NSLOT)]
        ringsem = [dsem('s_ring%d' % s) for s in range(NSLOT)]
        o_tmp = alloc(NTMP * TMPW)
        tmp = Pool_([T(f32v(o_tmp + i * TMPW, TMPW)) for i in range(NTMP)])
        o_bf = alloc(NBF * 256)
        bfp = Pool_([T(bfv(o_bf + i * 256, 256)) for i in range(NBF)])
        o_gp = alloc(NGP * 512)
        gpool = Pool_([T(f32v(o_gp + i * 512, 512)) for i in range(NGP)])
        o_xs = alloc(2 * 2048)
        xs = [f32v(o_xs + s * 2048, 2048) for s in range(2)]
        xsR = [Res() for _ in range(2)]
        xs_ld = [dsem('s_xld%d' % s) for s in range(2)]
        xs_st = [dsem('s_xst%d' % s) for s in range(2)]

        fst = [f32v(o_main + s * 8192, 8192).rearrange("p (k c) -> p k c", c=512) for s in range(2)]
        bst = [bfv(o_main + 16384 + s * 4096, 4096).rearrange("p (b k c) -> p b k c", b=4, k=16) for s in range(2)]
        assert o_main + 16384 + 8192 <= AW
        fstR = [Res() for _ in range(2)]
        bstR = [[Res() for _ in range(4)] for _ in range(2)]
        f_ld = [dsem('s_fld%d' % s) for s in range(2)]
        b_st = [[dsem('s_bst%d_%d' % (s, b)) for b in range(4)] for s in range(2)]

        bankR = [Res() for _ in range(8)]
        busy = [False] * 8
        bptr = [0]

        def alloc_bank():
            for k in range(8):
                b = (bptr[0] + k) % 8
                if not busy[b]:
                    busy[b] = True
                    bptr[0] = (b + 1) % 8
                    return b
            raise AssertionError("no free PSUM bank")

        def release(b):
            assert busy[b]
            busy[b] = False

        def barrier():
            for E in engines:
                for s in all_sems:
                    if s.val > 0 and E.waited.get(s, 0) < s.val:
                        E.waited[s] = s.val
                        E.prog.append(('w', s, s.val))

        dma(SP, par, par_d[:, :], par_sem, writes=[parR])
        op(DVE, lambda h: h.memset(onesD, 1.0 / 2048.0), writes=[constR])
        op(DVE, lambda h: h.memset(onesC, 1.0 / 1024.0), writes=[constR])
        op(DVE, lambda h: h.memset(epsR_ap, RMS_EPS), writes=[constR])
        op(DVE, lambda h: h.memset(epsL_ap, LN_EPS), writes=[constR])
        allst = [r for l in range(L) for j in range(8) for r in (stAR[l][j], stBR[l][j], stCR[l][j])]
        op(DVE, lambda h: h.memset(st_all, 0.0), writes=allst)

        unit_i = [0]

        def conv_unit(l, tname, src2d, K, ncol, c0):
            s = unit_i[0] % 2
            unit_i[0] += 1
            ncb = ncol // 128
            src = src2d.rearrange("(k p) c -> p k c", p=128)[:, :, c0:c0 + ncol]
            dma(SP, fst[s][:, 0:K, 0:ncol], src, f_ld[s], writes=[fstR[s]])
            for cb in range(ncb):
                E = ACT if cb % 2 == 0 else DVE
                o_ap = bst[s][:, cb, 0:K, :]
                i_ap = fst[s][:, 0:K, cb * 128:(cb + 1) * 128]
                if E is ACT:
                    op(E, lambda h, o=o_ap, i=i_ap: h.activation(out=o, in_=i, func=AF.Copy),
                       reads=[fstR[s]], writes=[bstR[s][cb]])
                else:
                    op(E, lambda h, o=o_ap, i=i_ap: h.tensor_copy(out=o, in_=i),
                       reads=[fstR[s]], writes=[bstR[s][cb]])
                p0 = jobpos[(tname, c0 // 128 + cb)]
                dst = wsc[l][:, p0 * 128:(p0 + K) * 128].rearrange("p (k c) -> p k c", c=128)
                dma(POOL, dst, o_ap, b_st[s][cb], reads=[bstR[s][cb]])

        for l in range(L):
            for u in range(IN_COLS // 512):
                conv_unit(l, 'w_in', w_in_d[l], 16, 512, u * 512)
            for g in range(4):
                conv_unit(l, 'gw%d' % g, gw_d[l, g], 2, 256, 0)
            for b in 'abc':
                for u in range(4):
                    conv_unit(l, 'w_br_' + b, w_br_d[b][l], 8, 512, u * 512)
            for u in range(4):
                conv_unit(l, 'w_o', w_o_d[l], 16, 512, u * 512)
        barrier()

        STATE_KINDS = ('au', 'ac', 'bg', 'bv', 'cu')
        used = set()
        for (kind_, idx_, src_) in jobs:
            if src_ is not None and kind_ in STATE_KINDS:
                p0_ = jobpos[(src_[0], src_[1])]
                for bb_ in range(p0_, p0_ + src_[2]):
                    used.add(bb_ // CH)
        chunk_list = []
        for gl_ in range(len(tiles) * L):
            for c_ in range(NCHUNK):
                if gl_ == L - 1 and c_ not in used:
                    continue
                chunk_list.append((gl_, c_))
        chunk_seq = {gc: i for i, gc in enumerate(chunk_list)}
        total_chunks = len(chunk_list)
        issued = [0]

        def issue_chunk(seq):
            gl, c = chunk_list[seq]
            l = gl % L
            slot = seq % NSLOT
            dma(SP, ring[slot], wsc[l][:, c * CH * 128:(c + 1) * CH * 128], ringsem[slot], writes=[ringR[slot]])

        def wblock(gl, b):
            seq = chunk_seq[(gl, b // CH)]
            lim = min(seq + NSLOT - 1, total_chunks - 1)
            while issued[0] <= lim:
                issue_chunk(issued[0])
                issued[0] += 1
            slot = seq % NSLOT
            return ring[slot][:, (b % CH) * 128:(b % CH) * 128 + 128], ringR[slot]

        def PSB(b, n):
            return ps[:, b, 0:n]

        pc = lambda i: par[:, i:i + 1]

        stg = [(f32v(o_y + 16 * 256, 2048), yinR[16:24], dsem('s_pf0')),
               (f32v(o_big, 2048), bigR[0:8], dsem('s_pf1')),
               (f32v(o_big + 2048, 2048), bigR[8:16], dsem('s_pf2')),
               (xs[0], [xsR[0]], xs_ld[0])]
        pre_issued = set()

        def issue_x(ti, sub):
            if (ti, sub) in pre_issued or ti >= len(tiles):
                return
            pre_issued.add((ti, sub))
            t0, n = tiles[ti]
            ap, rl, sem = stg[sub]
            dma(POOL, ap[:, :], xh[t0 + sub * 128:t0 + sub * 128 + 128, :], sem, writes=rl)

        def load_x(ti, t0, n):
            nsub = max(1, n // 128)
            for sub in range(nsub):
                nt = min(128, n)
                if n < 128:
                    src_ap, src_r = xs[0], [xsR[0]]
                    dma(POOL, xs[0][0:nt, :], xh[t0:t0 + nt, :], xs_ld[0], writes=[xsR[0]])
                else:
                    issue_x(ti, sub)
                    src_ap, src_r = stg[sub][0], stg[sub][1]
                for q4 in range(4):
                    b = alloc_bank()
                    for q in range(4):
                        c = q4 * 4 + q
                        op(PE, lambda h, o=ps[:, b, q * 128:q * 128 + nt], i=src_ap[0:nt, c * 128:(c + 1) * 128], idn=ident[0:nt, 0:nt]:
                           h.transpose(out=o, in_=i, identity=idn),
                           reads=src_r + [parR], writes=[bankR[b]])
                    o_ap = xT[:, q4 * 4:q4 * 4 + 4, sub * 128:sub * 128 + nt]
                    i_ap = ps[:, b, :].rearrange("p (q t) -> p q t", t=128)[:, :, 0:nt]
                    wr = [xTR[q4 * 4 + q] for q in range(4)]
                    if q4 % 2 == 0:
                        op(ACT, lambda h, o=o_ap, i=i_ap: h.activation(out=o, in_=i, func=AF.Copy),
                           reads=[bankR[b]], writes=wr)
                    else:
                        op(DVE, lambda h, o=o_ap, i=i_ap: h.tensor_copy(out=o, in_=i),
                           reads=[bankR[b]], writes=wr)
                    release(b)

        def rms_stat(b, c, n, sq):
            op(PE, lambda h, o=PSB(b, n), r=sq.ap[:, 0:n], c=c: h.matmul(o, onesD, r, start=(c == 0), stop=(c == 15)),
               reads=[sq.res, constR], writes=[bankR[b]])
            bfp.free(sq)

        def rms_sq(c, n):
            sq = bfp.get()
            op(ACT, lambda h, o=sq.ap[:, 0:n], i=xT[:, c, 0:n]: h.activation(out=o, in_=i, func=AF.Square),
               reads=[xTR[c]], writes=[sq.res])
            return sq

        def rms(gcol, n, final, b=None):
            if b is None:
                b = alloc_bank()
                for c in range(16):
                    rms_stat(b, c, n, rms_sq(c, n))
            sd = tmp.get()
            op(ACT, lambda h, o=sd.ap[:, 0:n], i=PSB(b, n): h.activation(out=o, in_=i, func=AF.Sqrt, bias=epsR_ap, scale=1.0),
               reads=[bankR[b], constR], writes=[sd.res])
            release(b)
            rs = tmp.get()
            op(DVE, lambda h, o=rs.ap[:, 0:n], i=sd.ap[:, 0:n]: h.reciprocal(out=o, in_=i), reads=[sd.res], writes=[rs.res])
            tmp.free(sd)
            for c in range(16):
                if final:
                    o_ap, wr = oT[:, c, 0:n], oTR(c)
                else:
                    o_ap, wr = hT[:, c, 0:n], [hTR[c]]
                op(DVE, lambda h, o=o_ap, i=xT[:, c, 0:n], sc=pc(gcol + c), r=rs.ap[:, 0:n]:
                   h.scalar_tensor_tensor(out=o, in0=i, scalar=sc, in1=r, op0=ALU.mult, op1=ALU.mult),
                   reads=[xTR[c], rs.res, parR], writes=wr)
            tmp.free(rs)

        def store_out(orow0, n, rc=None):
            for sub in range(n // 128):
                s = sub % 2
                for q4 in range(4):
                    b = alloc_bank()
                    for q in range(4):
                        c = q4 * 4 + q
                        op(PE, lambda h, o=ps[:, b, q * 128:(q + 1) * 128], i=oT[:, c, sub * 128:(sub + 1) * 128]:
                           h.transpose(out=o, in_=i, identity=ident),
                           reads=oTR(c) + [parR], writes=[bankR[b]])
                    o_ap = xs[s][:, q4 * 512:(q4 + 1) * 512]
                    sc = rc.ap[:, sub:sub + 1]
                    if q4 % 2 == 0:
                        op(ACT, lambda h, o=o_ap, i=ps[:, b, :], sc=sc: h.activation(out=o, in_=i, func=AF.Copy, scale=sc),
                           reads=[bankR[b], rc.res], writes=[xsR[s]])
                    else:
                        op(DVE, lambda h, o=o_ap, i=ps[:, b, :], sc=sc: h.tensor_scalar(out=o, in0=i, scalar1=sc, scalar2=None, op0=ALU.mult),
                           reads=[bankR[b], rc.res], writes=[xsR[s]])
                    release(b)
                dma(POOL, out_d[orow0 + sub * 128:orow0 + (sub + 1) * 128, :], xs[s][:, :], xs_st[s], reads=[xsR[s]])

        def final_out(rb, orow0, n):
            row = tmp.get()
            op(ACT, lambda h, o=row.ap[0:1, 0:n], i=ps[0:1, rb, 0:n]: h.activation(out=o, in_=i, func=AF.Copy),
               reads=[bankR[rb]], writes=[row.res])
            release(rb)
            bc = alloc_bank()
            for sub in range(n // 128):
                op(PE, lambda h, o=ps[:, bc, sub:sub + 1], i=row.ap[0:1, sub * 128:(sub + 1) * 128]:
                   h.transpose(out=o, in_=i, identity=ident[0:1, 0:1]),
                   reads=[row.res, parR], writes=[bankR[bc]])
            nsub = n // 128
            sd = tmp.get()
            op(ACT, lambda h, o=sd.ap[:, 0:nsub], i=ps[:, bc, 0:nsub]: h.activation(out=o, in_=i, func=AF.Sqrt, bias=epsR_ap, scale=1.0),
               reads=[bankR[bc], constR], writes=[sd.res])
            release(bc)
            tmp.free(row)
            rc = tmp.get()
            op(DVE, lambda h, o=rc.ap[:, 0:nsub], i=sd.ap[:, 0:nsub]: h.reciprocal(out=o, in_=i), reads=[sd.res], writes=[rc.res])
            tmp.free(sd)
            store_out(orow0, n, rc)
            tmp.free(rc)

        def run_layer(ti, l, n, state_only=False):
            gl = ti * L + l
            S = {}

            def rhs_for(kind, idx, kc):
                if kind == 'ya':
                    return yin[:, kc, 0:n], yinR[kc]
                if kind == 'yb':
                    return yin[:, 8 + kc, 0:n], yinR[8 + kc]
                if kind == 'yc':
                    return yin[:, 16 + kc, 0:n], yinR[16 + kc]
                if kind == 'o':
                    return mg[:, kc, 0:n], bigR[kc]
                if kind == 'gy':
                    d = S['d', (idx // 2) * 2 + kc]
                    return d.ap[:, 0:n], d.res
                return hT[:, kc, 0:n], hTR[kc]

            def wjob(kind, idx, src):
                b = alloc_bank()
                tname, cb, K = src
                p0 = jobpos[(tname, cb)]
                for kc in range(K):
                    wap, wres = wblock(gl, p0 + kc)
                    r_ap, r_res = rhs_for(kind, idx, kc)
                    op(PE, lambda h, o=PSB(b, n), w=wap, r=r_ap, kc=kc, K=K: h.matmul(o, w, r, start=(kc == 0), stop=(kc == K - 1)),
                       reads=[wres, r_res], writes=[bankR[b]])
                return b

            pend = []

            def drain(k):
                for _ in range(min(k, len(pend))):
                    pend.pop(0)[1]()

            def drain_upto(jmax):
                while pend and pend[0][0] <= jmax:
                    pend.pop(0)[1]()

            def act(fn, reads, writes):
                op(ACT, fn, reads, writes)

            def dve(fn, reads, writes):
                op(DVE, fn, reads, writes)

            def bz_stage1(j):
                rstd, nmr = S['rstd'], S['nmr']
                vres = [bigR[2 * j], bigR[2 * j + 1]]
                c1 = tmp.get()
                dve(lambda h, o=c1.ap[:, 0:n], a=vc[:, j, 0:n], r=rstd.ap[:, 0:n]: h.tensor_tensor(out=o, in0=a, in1=r, op=ALU.mult),
                    vres + [rstd.res], [c1.res])
                dve(lambda h, o=c1.ap[:, 0:n], r=nmr.ap[:, 0:n]: h.tensor_tensor(out=o, in0=o, in1=r, op=ALU.add),
                    [c1.res, nmr.res], [c1.res])
                act(lambda h, o=c1.ap[:, 0:n], g=pc(P_LG + l * 8 + j), bb=pc(P_LB + l * 8 + j):
                    h.activation(out=o, in_=o, func=AF.Silu, bias=bb, scale=g),
                    [c1.res, parR], [c1.res])
                S['c1', j] = c1

            def bz_consume(j, b):
                c1 = S.pop(('c1', j))
                sz = tmp.get()
                act(lambda h, o=sz.ap[:, 0:n], i=PSB(b, n): h.activation(out=o, in_=i, func=AF.Silu), [bankR[b]], [sz.res])
                release(b)
                dve(lambda h, o=yin[:, 8 + j, 0:n], a=c1.ap[:, 0:n], s=sz.ap[:, 0:n]: h.tensor_tensor(out=o, in0=a, in1=s, op=ALU.mult),
                    [c1.res, sz.res], [yinR[8 + j]])
                tmp.free(c1, sz)
                if j + BZ_AHEAD < 8:
                    bz_stage1(j + BZ_AHEAD)
                if j == 7:
                    tmp.free(S.pop('rstd'), S.pop('nmr'))

            for (kind, idx, src) in jobs:
                j = idx
                if state_only:
                    if kind not in STATE_KINDS:
                        continue
                    b = wjob(kind, idx, src)
                    if kind == 'au' or kind == 'bg':
                        t_ = tmp.get()
                        fn_ = AF.Copy if kind == 'au' else AF.Sigmoid
                        act(lambda h, o=t_.ap[:, 0:n], i=PSB(b, n), fn_=fn_: h.activation(out=o, in_=i, func=fn_), [bankR[b]], [t_.res])
                        release(b)
                        S['so'] = t_
                    elif kind == 'ac' or kind == 'bv':
                        t_ = S.pop('so')
                        H_, st_, stR_ = (8, stA, stAR) if kind == 'ac' else (30, stB, stBR)
                        mb = tmp.get()
                        dve(lambda h, o=mb.ap[:, H_:H_ + n], a=PSB(b, n), u=t_.ap[:, 0:n]: h.tensor_tensor(out=o, in0=a, in1=u, op=ALU.mult),
                            [bankR[b], t_.res], [mb.res])
                        release(b)
                        tmp.free(t_)
                        op(DVE, lambda h, o=st_(l, j), i=mb.ap[:, n:n + H_]: h.tensor_copy(out=o, in_=i), [mb.res], [stR_[l][j]])
                        tmp.free(mb)
                    else:
                        ub = tmp.get()
                        act(lambda h, o=ub.ap[:, 15:15 + n], i=PSB(b, n): h.activation(out=o, in_=i, func=AF.Copy), [bankR[b]], [ub.res])
                        release(b)
                        op(DVE, lambda h, o=stC(l, j), i=ub.ap[:, n:n + 15]: h.tensor_copy(out=o, in_=i), [ub.res], [stCR[l][j]])
                        tmp.free(ub)
                    continue
                if kind == 'st':
                    drain_upto(j)
                    vcb, sqb = bfp.get(), bfp.get()
                    vres_ = [bigR[2 * j], bigR[2 * j + 1]]
                    act(lambda h, o=vcb.ap[:, 0:n], i=vc[:, j, 0:n]: h.activation(out=o, in_=i, func=AF.Copy), vres_, [vcb.res])
                    act(lambda h, o=sqb.ap[:, 0:n], i=vc[:, j, 0:n]: h.activation(out=o, in_=i, func=AF.Square), vres_, [sqb.res])
                    if j == 0:
                        S['bm'], S['be'] = alloc_bank(), alloc_bank()
                    bm, be = S['bm'], S['be']
                    op(PE, lambda h, o=PSB(bm, n), r=vcb.ap[:, 0:n], j=j: h.matmul(o, onesC, r, start=(j == 0), stop=(j == 7)),
                       reads=[vcb.res, constR], writes=[bankR[bm]])
                    op(PE, lambda h, o=PSB(be, n), r=sqb.ap[:, 0:n], j=j: h.matmul(o, onesC, r, start=(j == 0), stop=(j == 7)),
                       reads=[sqb.res, constR], writes=[bankR[be]])
                    bfp.free(vcb, sqb)
                    if j == 7:
                        msb, m2, rstd = tmp.get(), tmp.get(), tmp.get()
                        act(lambda h, o=msb.ap[:, 0:n], i=PSB(bm, n): h.activation(out=o, in_=i, func=AF.Copy), [bankR[bm]], [msb.res])
                        dve(lambda h, o=m2.ap[:, 0:n], a=msb.ap[:, 0:n]: h.tensor_tensor(out=o, in0=a, in1=a, op=ALU.mult), [msb.res], [m2.res])
                        dve(lambda h, o=m2.ap[:, 0:n], a=PSB(be, n): h.tensor_tensor(out=o, in0=a, in1=o, op=ALU.subtract),
                            [bankR[be], m2.res], [m2.res])
                        release(bm)
                        release(be)
                        S.pop('bm')
                        S.pop('be')
                        act(lambda h, o=m2.ap[:, 0:n]: h.activation(out=o, in_=o, func=AF.Sqrt, bias=epsL_ap, scale=1.0),
                            [m2.res, constR], [m2.res])
                        dve(lambda h, o=rstd.ap[:, 0:n], i=m2.ap[:, 0:n]: h.reciprocal(out=o, in_=i), [m2.res], [rstd.res])
                        dve(lambda h, o=msb.ap[:, 0:n], r=rstd.ap[:, 0:n]:
                            h.scalar_tensor_tensor(out=o, in0=o, scalar=-1.0, in1=r, op0=ALU.mult, op1=ALU.mult),
                            [msb.res, rstd.res], [msb.res])
                        tmp.free(m2)
                        S['rstd'], S['nmr'] = rstd, msb
                        for jj in range(BZ_AHEAD):
                            bz_stage1(jj)
                        for (jj, bb) in S.pop('bz_pending'):
                            bz_consume(jj, bb)
                    continue

                b = wjob(kind, idx, src)
                drain(DRAIN_K)

                if kind == 'au':
                    au = tmp.get()
                    act(lambda h, o=au.ap[:, 0:n], i=PSB(b, n): h.activation(out=o, in_=i, func=AF.Copy), [bankR[b]], [au.res])
                    release(b)
                    S['au'] = au
                elif kind == 'ac':
                    au = S.pop('au')
                    mb = tmp.get()
                    op(DVE, lambda h, o=mb.ap[:, 0:8], i=stA(l, j): h.tensor_copy(out=o, in_=i), [stAR[l][j]], [mb.hres])
                    dve(lambda h, o=mb.ap[:, 8:8 + n], a=PSB(b, n), u=au.ap[:, 0:n]: h.tensor_tensor(out=o, in0=a, in1=u, op=ALU.mult),
                        [bankR[b], au.res], [mb.res])
                    release(b)
                    tmp.free(au)
                    op(DVE, lambda h, o=stA(l, j), i=mb.ap[:, n:n + 8]: h.tensor_copy(out=o, in_=i), [mb.res], [stAR[l][j]])
                    cv = tmp.get()
                    wc = P_AW + (l * 8 + j) * 3
                    dve(lambda h, o=cv.ap[:, 0:n], i=mb.ap[:, 6:6 + n], w=pc(wc):
                        h.tensor_scalar(out=o, in0=i, scalar1=w, scalar2=None, op0=ALU.mult),
                        [mb.res, mb.hres, parR], [cv.res])
                    for k in (1, 2):
                        dve(lambda h, o=cv.ap[:, 0:n], i=mb.ap[:, 6 + k:6 + k + n], w=pc(wc + k):
                            h.scalar_tensor_tensor(out=o, in0=i, scalar=w, in1=o, op0=ALU.mult, op1=ALU.add),
                            [mb.res, mb.hres, cv.res, parR], [cv.res])
                    tmp.free(mb)
                    S['cv'] = cv
                elif kind == 'ab':
                    cv = S.pop('cv')
                    y1 = tmp.get()
                    dve(lambda h, o=y1.ap[:, 0:n], a=PSB(b, n), c=cv.ap[:, 0:n]: h.tensor_tensor(out=o, in0=a, in1=c, op=ALU.mult),
                        [bankR[b], cv.res], [y1.res])
                    release(b)
                    tmp.free(cv)
                    S['y1'] = y1
                elif kind == 'az':
                    y1 = S.pop('y1')
                    sz = tmp.get()
                    act(lambda h, o=sz.ap[:, 0:n], i=PSB(b, n): h.activation(out=o, in_=i, func=AF.Silu), [bankR[b]], [sz.res])
                    release(b)
                    dve(lambda h, o=yin[:, j, 0:n], a=y1.ap[:, 0:n], s=sz.ap[:, 0:n]: h.tensor_tensor(out=o, in0=a, in1=s, op=ALU.mult),
                        [y1.res, sz.res], [yinR[j]])
                    tmp.free(y1, sz)
                elif kind == 'bg':
                    sg = tmp.get()
                    act(lambda h, o=sg.ap[:, 0:n], i=PSB(b, n): h.activation(out=o, in_=i, func=AF.Sigmoid), [bankR[b]], [sg.res])
                    release(b)
                    S['sg'] = sg
                elif kind == 'bv':
                    sg = S.pop('sg')
                    vb = tmp.get()
                    op(DVE, lambda h, o=vb.ap[:, 0:30], i=stB(l, j): h.tensor_copy(out=o, in_=i), [stBR[l][j]], [vb.hres])
                    dve(lambda h, o=vb.ap[:, 30:30 + n], a=PSB(b, n), s=sg.ap[:, 0:n]: h.tensor_tensor(out=o, in0=a, in1=s, op=ALU.mult),
                        [bankR[b], sg.res], [vb.res])
                    release(b)
                    tmp.free(sg)
                    op(DVE, lambda h, o=stB(l, j), i=vb.ap[:, n:n + 30]: h.tensor_copy(out=o, in_=i), [vb.res], [stBR[l][j]])
                    vres = [bigR[2 * j], bigR[2 * j + 1]]
                    wc = P_BW + (l * 8 + j) * 31
                    dve(lambda h, o=vc[:, j, 0:n], i=vb.ap[:, 0:n], w=pc(wc), bb=pc(P_BB + l * 8 + j):
                        h.tensor_scalar(out=o, in0=i, scalar1=w, scalar2=bb, op0=ALU.mult, op1=ALU.add),
                        [vb.res, vb.hres, parR], vres)
                    a1 = tmp.get()
                    pend.append((j, lambda vb=vb, wc=wc, a1=a1: dve(
                        lambda h, o=a1.ap[:, 0:n], i=vb.ap[:, 1:1 + n], w=pc(wc + 1):
                        h.tensor_scalar(out=o, in0=i, scalar1=w, scalar2=None, op0=ALU.mult),
                        [vb.res, vb.hres, parR], [a1.res])))
                    for k in range(2, 31):
                        if k % 2 == 0:
                            pend.append((j, lambda j=j, k=k, vb=vb, wc=wc, vres=vres: dve(
                                lambda h, o=vc[:, j, 0:n], i=vb.ap[:, k:k + n], w=pc(wc + k):
                                h.scalar_tensor_tensor(out=o, in0=i, scalar=w, in1=o, op0=ALU.mult, op1=ALU.add),
                                [vb.res, vb.hres, parR] + vres, vres)))
                        else:
                            pend.append((j, lambda k=k, vb=vb, wc=wc, a1=a1: dve(
                                lambda h, o=a1.ap[:, 0:n], i=vb.ap[:, k:k + n], w=pc(wc + k):
                                h.scalar_tensor_tensor(out=o, in0=i, scalar=w, in1=o, op0=ALU.mult, op1=ALU.add),
                                [vb.res, vb.hres, parR, a1.res], [a1.res])))
                    pend.append((j, lambda j=j, a1=a1, vres=vres: dve(
                        lambda h, o=vc[:, j, 0:n], a=a1.ap[:, 0:n]: h.tensor_tensor(out=o, in0=o, in1=a, op=ALU.add),
                        [a1.res] + vres, vres)))
                    pend.append((j, lambda vb=vb, a1=a1: tmp.free(vb, a1)))
                elif kind == 'bz':
                    if 'rstd' not in S:
                        S.setdefault('bz_pending', []).append((j, b))
                    else:
                        bz_consume(j, b)
                elif kind == 'cu':
                    g = j // 2
                    w = 2 << g
                    ub = tmp.get()
                    op(DVE, lambda h, o=ub.ap[:, 0:15], i=stC(l, j): h.tensor_copy(out=o, in_=i), [stCR[l][j]], [ub.hres])
                    act(lambda h, o=ub.ap[:, 15:15 + n], i=PSB(b, n): h.activation(out=o, in_=i, func=AF.Copy), [bankR[b]], [ub.res])
                    release(b)
                    op(DVE, lambda h, o=stC(l, j), i=ub.ap[:, n:n + 15]: h.tensor_copy(out=o, in_=i), [ub.res], [stCR[l][j]])
                    c0 = 15 - (w - 2)
                    ln = n + w - 2
                    cur = tmp.get()
                    dve(lambda h, o=cur.ap[:, 0:ln], a=ub.ap[:, c0:c0 + ln], bb=ub.ap[:, c0 - 1:c0 - 1 + ln]:
                        h.tensor_tensor(out=o, in0=a, in1=bb, op=ALU.add), [ub.res, ub.hres], [cur.res])
                    step = 2
                    while step < w:
                        ln -= step
                        nxt = tmp.get()
                        dve(lambda h, o=nxt.ap[:, 0:ln], a=cur.ap[:, step:step + ln], bb=cur.ap[:, 0:ln]:
                            h.tensor_tensor(out=o, in0=a, in1=bb, op=ALU.add), [cur.res], [nxt.res])
                        tmp.free(cur)
                        cur = nxt
                        step *= 2
                    assert ln == n
                    d = bfp.get()
                    dve(lambda h, o=d.ap[:, 0:n], s=cur.ap[:, 0:n], u=ub.ap[:, 15:15 + n], iw=1.0 / w:
                        h.scalar_tensor_tensor(out=o, in0=s, scalar=iw, in1=u, op0=ALU.mult, op1=ALU.subtract),
                        [cur.res, ub.res], [d.res])
                    if ti == 1:
                        fx = tmp.get()
                        dve(lambda h, o=fx.ap[:, 0:16], s=cur.ap[:, 0:16], ic=par[:, P_IC + g * 16:P_IC + g * 16 + 16]:
                            h.tensor_tensor(out=o, in0=s, in1=ic, op=ALU.mult), [cur.res, parR], [fx.res])
                        dve(lambda h, o=d.ap[:, 0:16], a=fx.ap[:, 0:16], u=ub.ap[:, 15:31]:
                            h.tensor_tensor(out=o, in0=a, in1=u, op=ALU.subtract), [fx.res, ub.res, d.res], [d.res])
                        tmp.free(fx)
                    tmp.free(cur, ub)
                    S['d', j] = d
                elif kind == 'cz':
                    szc = tmp.get()
                    act(lambda h, o=szc.ap[:, 0:n], i=PSB(b, n): h.activation(out=o, in_=i, func=AF.Silu), [bankR[b]], [szc.res])
                    release(b)
                    S['szc', j] = szc
                elif kind == 'gy':
                    szc = S.pop(('szc', j))
                    dve(lambda h, o=yin[:, 16 + j, 0:n], a=PSB(b, n), sc=pc(P_CS + l * 8 + j), s=szc.ap[:, 0:n]:
                        h.scalar_tensor_tensor(out=o, in0=a, scalar=sc, in1=s, op0=ALU.mult, op1=ALU.mult),
                        [bankR[b], szc.res, parR], [yinR[16 + j]])
                    release(b)
                    tmp.free(szc)
                    if j % 2 == 1:
                        bfp.free(S.pop(('d', j - 1)), S.pop(('d', j)))
                elif kind in ('g0', 'g1', 'g2'):
                    pool_ = gpool if j in HOIST else tmp
                    sg = pool_.get()
                    act(lambda h, o=sg.ap[:, 0:n], i=PSB(b, n): h.activation(out=o, in_=i, func=AF.Sigmoid), [bankR[b]], [sg.res])
                    release(b)
                    S[kind, j] = (sg, pool_)
                elif kind == 'ya':
                    sg, pool_ = S.pop(('g0', j))
                    acc = tmp.get()
                    dve(lambda h, o=acc.ap[:, 0:n], a=PSB(b, n), s=sg.ap[:, 0:n]: h.tensor_tensor(out=o, in0=a, in1=s, op=ALU.mult),
                        [bankR[b], sg.res], [acc.res])
                    release(b)
                    pool_.free(sg)
                    S['acc'] = acc
                elif kind in ('yb', 'yc'):
                    sg, pool_ = S.pop(('g1' if kind == 'yb' else 'g2', j))
                    acc = S['acc']
                    p = tmp.get()
                    dve(lambda h, o=p.ap[:, 0:n], a=PSB(b, n), s=sg.ap[:, 0:n]: h.tensor_tensor(out=o, in0=a, in1=s, op=ALU.mult),
                        [bankR[b], sg.res], [p.res])
                    release(b)
                    pool_.free(sg)
                    if kind == 'yb':
                        dve(lambda h, o=acc.ap[:, 0:n], pp=p.ap[:, 0:n]: h.tensor_tensor(out=o, in0=o, in1=pp, op=ALU.add),
                            [acc.res, p.res], [acc.res])
                        tmp.free(p)
                    else:
                        dve(lambda h, o=mg[:, j, 0:n], a=acc.ap[:, 0:n], pp=p.ap[:, 0:n]: h.tensor_tensor(out=o, in0=a, in1=pp, op=ALU.add),
                            [acc.res, p.res], [bigR[j]])
                        tmp.free(p, acc)
                        S.pop('acc')
                elif kind == 'o':
                    if j == 0:
                        S['rb'] = alloc_bank()
                        S['rq'] = []
                        if l == L - 1:
                            issue_x(ti + 1, 0)
                    dve(lambda h, o=xT[:, j, 0:n], a=PSB(b, n): h.tensor_tensor(out=o, in0=a, in1=o, op=ALU.add),
                        [bankR[b], xTR[j]], [xTR[j]])
                    release(b)
                    S['rq'].append((j, rms_sq(j, n)))
                    if l == L - 1 and ti > 0:
                        act(lambda h, o=oT[:, j, 0:n], i=xT[:, j, 0:n], g=pc(P_FG + j): h.activation(out=o, in_=i, func=AF.Copy, scale=g),
                            [xTR[j], parR], oTR(j))
                    while S['rq'] and S['rq'][0][0] <= j - 2:
                        c, sq = S['rq'].pop(0)
                        rms_stat(S['rb'], c, n, sq)
                else:
                    raise AssertionError(kind)
            if state_only:
                assert not S and not pend
                return None
            for c, sq in S.pop('rq'):
                rms_stat(S['rb'], c, n, sq)
            rb = S.pop('rb')
            if l == L - 1:
                issue_x(ti + 1, 1)
                issue_x(ti + 1, 2)
            assert not S, list(S.keys())
            assert not pend
            return rb

        for ti, (t0, n) in enumerate(tiles):
            load_x(ti, t0, n)
            rb = None
            for l in range(L):
                rms(P_NG + l * 16, n, final=False, b=rb)
                rb = run_layer(ti, l, n, state_only=(ti == 0 and l == L - 1))
            if ti > 0:
                final_out(rb, t0 - HALO, n)
        for s in range(2):
            POOL.prog.append(('w', xs_st[s], xs_st[s].val))
        assert issued[0] == total_chunks

        block = es.enter_context(nc.Block())

        def runner(E):
            def body(h):
                for it in E.prog:
                    if it[0] == 'w':
                        h.wait_ge(it[1].h, it[2])
                    elif it[0] == 'i':
                        it[1](h).then_inc(E.sem.h, 1)
                    else:
                        h.dma_start(out=it[1], in_=it[2]).then_inc(it[3].h, 16)
            return body
        block.sync(runner(SP))
        block.tensor(runner(PE))
        block.scalar(runner(ACT))
        block.vector(runner(DVE))
        block.gpsimd(runner(POOL))
    return nc


def pack_params(core, seq_start, norm_g, final_g, a_conv_w, b_conv_w, b_conv_b, b_ln_g, b_ln_b, c_scale):
    p = np.zeros((128, P_N), np.float32)
    p[:, P_NG:P_NG + 32] = norm_g.reshape(2, 16, 128).transpose(2, 0, 1).reshape(128, 32)
    p[:, P_FG:P_FG + 16] = final_g.reshape(16, 128).T
    p[:, P_AW:P_AW + 48] = a_conv_w.reshape(2, 3, 8, 128).transpose(3, 0, 2, 1).reshape(128, 48)
    p[:, P_BW:P_BW + 496] = b_conv_w.reshape(2, 31, 8, 128).transpose(3, 0, 2, 1).reshape(128, 496)
    for off, v in ((P_BB, b_conv_b), (P_LG, b_ln_g), (P_LB, b_ln_b), (P_CS, c_scale)):
        p[:, off:off + 16] = v.reshape(2, 8, 128).transpose(2, 0, 1).reshape(128, 16)
    ic = np.zeros((4, 16), np.float32)
    for g in range(4):
        w = 2 << g
        for t in range(16):
            ic[g, t] = 1.0 / (min(t + 1, w) if seq_start else w)
    p[:, P_IC:P_IC + 64] = ic.reshape(1, 64)
    p[:, P_ID:P_ID + 128] = np.eye(128, dtype=np.float32)
    return p


_NC_CACHE = {}


def kernel(x, norm_g, w_in, a_conv_w, b_conv_w, b_conv_b, b_ln_g, b_ln_b,
           c_group_w, c_scale, w_br_a, w_br_b, w_br_c, w_o, final_g):
    x = np.asarray(x, np.float32)
    B, S, _ = x.shape
    per_seq = NCORES // B
    tpc = S // per_seq
    if tpc not in _NC_CACHE:
        _NC_CACHE[tpc] = build_program(tpc)
    nc = _NC_CACHE[tpc]
    f = lambda a: np.ascontiguousarray(np.asarray(a, np.float32))
    w_in, w_br_a, w_br_b, w_br_c, w_o, c_group_w = map(f, (w_in, w_br_a, w_br_b, w_br_c, w_o, c_group_w))
    smalls = [f(a) for a in (norm_g, final_g, a_conv_w, b_conv_w, b_conv_b, b_ln_g, b_ln_b, c_scale)]
    in_maps = []
    for core in range(NCORES):
        b, q = divmod(core, per_seq)
        s0 = q * tpc
        xhh = np.zeros((HALO + tpc, D), np.float32)
        if s0 == 0:
            xhh[HALO:] = x[b, 0:tpc]
        else:
            xhh[:] = x[b, s0 - HALO:s0 + tpc]
        in_maps.append({
            "xh": xhh, "par": pack_params(core, s0 == 0, *smalls),
            "w_in": w_in, "w_br_a": w_br_a, "w_br_b": w_br_b, "w_br_c": w_br_c,
            "w_o": w_o, "c_group_w": c_group_w,
        })
    res = run_bass_kernel_spmd(nc, in_maps, core_ids=list(range(NCORES)))
    out = np.empty((B, S, D), np.float32)
    for core in range(NCORES):
        b, q = divmod(core, per_seq)
        out[b, q * tpc:(q + 1) * tpc] = res.results[core]["out"]
    return out
```

```python
import numpy as np
import concourse.bass as bass
import concourse.mybir as mybir
from concourse.bass_utils import run_bass_kernel_spmd
from contextlib import ExitStack

F32 = mybir.dt.float32
BF16 = mybir.dt.bfloat16
AF = mybir.ActivationFunctionType
ALU = mybir.AluOpType

D = 2048
NCH = 16
L = 2
IN_COLS = 15360
RMS_EPS = 1e-6
LN_EPS = 1e-5
HALO = 64
TILE = 512
NCORES = 8
COL = dict(au=0, ab=1024, ac=2048, az=3072, bv=4096, bg=5120, bz=6144, cu=7168, cz=8192,
           g0=9216, g1=11264, g2=13312)
CH = 56
NSLOT = 3
HOIST = (1, 2, 3)
NGP = 9
NBLK = 2576
NCHUNK = NBLK // CH
TMPW = 544
NTMP = 12
NBF = 10
DRAIN_K = 5
BZ_AHEAD = 3

P_NG, P_FG, P_AW, P_BW, P_BB, P_LG, P_LB, P_CS, P_IC, P_ID, P_N = 0, 32, 48, 96, 592, 608, 624, 640, 656, 720, 848


def layer_jobs():
    jobs = []

    def win(kind, j):
        return (kind, j, ('w_in', COL[kind] // 128 + j, 16))

    def A(j):
        return [win('au', j), win('ac', j), win('ab', j), win('az', j)]

    def B1(j):
        return [win('bg', j), win('bv', j)]

    def C(g):
        return [win('cu', 2 * g), win('cu', 2 * g + 1), win('cz', 2 * g), win('cz', 2 * g + 1)]

    def G(g):
        return [('gy', 2 * g + m, ('gw%d' % g, m, 2)) for m in (0, 1)]

    hoisted = [win(g, i) for i in HOIST for g in ('g0', 'g1', 'g2')]
    hq = list(hoisted)
    for j in range(8):
        jobs += A(j) + B1(j)
        if hq:
            jobs.append(hq.pop(0))
        if j % 2 == 1:
            jobs += C(j // 2)
            if hq and len(hq) > 7 - j:
                jobs.append(hq.pop(0))
        if j >= 1:
            jobs.append(('st', j - 1, None))
        if j % 2 == 0 and j >= 2:
            jobs += G(j // 2 - 1)
    jobs += [win('bz', 0), win('bz', 1)] + G(3) + [win('g0', 0), win('g1', 0), win('g2', 0)] + [('st', 7, None)]
    for j in range(2, 8):
        jobs.append(win('bz', j))
    for i in range(16):
        if i > 0 and i not in HOIST:
            jobs += [win('g0', i), win('g1', i), win('g2', i)]
        jobs += [('ya', i, ('w_br_a', i, 8)), ('yb', i, ('w_br_b', i, 8)), ('yc', i, ('w_br_c', i, 8))]
    for i in range(16):
        jobs.append(('o', i, ('w_o', i, 16)))
    pos = {}
    p = 0
    for (_, _, src) in jobs:
        if src is not None:
            pos[(src[0], src[1])] = p
            p += src[2]
    assert p == NBLK, p
    return jobs, pos


class Sem:
    def __init__(self, h):
        self.h = h
        self.val = 0


class Eng:
    def __init__(self, name, sem):
        self.name = name
        self.sem = sem
        self.prog = []
        self.waited = {}


class Res:
    __slots__ = ('w', 'weng', 'rd')

    def __init__(self):
        self.w = None
        self.weng = None
        self.rd = {}


class T:
    def __init__(self, ap):
        self.ap = ap
        self.res = Res()
        self.hres = Res()


class Pool_:
    def __init__(self, tiles):
        self.free_list = list(tiles)

    def get(self):
        assert self.free_list, "scratch pool exhausted"
        return self.free_list.pop(0)

    def free(self, *ts):
        for t in ts:
            assert t not in self.free_list
            self.free_list.append(t)


def _need(E, reads, writes, relax=False):
    need = {}

    def add(tok):
        s, v = tok
        if need.get(s, 0) < v:
            need[s] = v
    for r in reads:
        if r.w is not None:
            if relax and E.name == 'DVE' and r.weng == 'DVE' and r.w[1] <= E.sem.val - 1 and getattr(E, 'last_big', False):
                continue
            add(r.w)
    for w in writes:
        if w.w is not None and w.weng != E.name:
            add(w.w)
        for en, tok in w.rd.items():
            if en != E.name:
                add(tok)
    for s, v in need.items():
        if E.waited.get(s, 0) < v:
            E.waited[s] = v
            E.prog.append(('w', s, v))


def op(E, fn, reads=(), writes=(), relax=False, big=False):
    _need(E, reads, writes, relax)
    E.last_big = big
    E.sem.val += 1
    tok = (E.sem, E.sem.val)
    E.prog.append(('i', fn))
    for r in reads:
        r.rd[E.name] = tok
    for w in writes:
        w.w = tok
        w.weng = E.name
        w.rd = {}


def dma(Q, out, in_, sem, reads=(), writes=()):
    _need(Q, reads, writes)
    sem.val += 16
    tok = (sem, sem.val)
    Q.prog.append(('d', out, in_, sem))
    for r in reads:
        r.rd[('dma', id(sem))] = tok
    for w in writes:
        w.w = tok
        w.weng = 'dma'
        w.rd = {}


def build_program(tok_per_core):
    ntiles = tok_per_core // TILE
    tiles = [(0, HALO)] + [(HALO + i * TILE, TILE) for i in range(ntiles)]
    jobs, jobpos = layer_jobs()

    nc = bass.Bass("TRN2", target_bir_lowering=False)
    xh = nc.dram_tensor("xh", [HALO + tok_per_core, D], F32, kind="ExternalInput").ap()
    par_d = nc.dram_tensor("par", [128, P_N], F32, kind="ExternalInput").ap()
    w_in_d = nc.dram_tensor("w_in", [L, D, IN_COLS], F32, kind="ExternalInput").ap()
    w_br_d = {b: nc.dram_tensor("w_br_" + b, [L, 1024, D], F32, kind="ExternalInput").ap() for b in 'abc'}
    w_o_d = nc.dram_tensor("w_o", [L, D, D], F32, kind="ExternalInput").ap()
    gw_d = nc.dram_tensor("c_group_w", [L, 4, 256, 256], F32, kind="ExternalInput").ap()
    out_d = nc.dram_tensor("out", [tok_per_core, D], F32, kind="ExternalOutput").ap()
    wsc = nc.dram_tensor("wsc", [L, 128, NBLK * 128], BF16, kind="Internal").ap()

    es = ExitStack()
    with es:
        AW = 53100
        arena = es.enter_context(nc.sbuf_tensor("arena", [128, AW], F32))
        ps = es.enter_context(nc.psum_tensor("ps", [128, 8, 512], F32))
        _off = [0]

        def alloc(words):
            o = _off[0]
            _off[0] += words
            assert _off[0] <= AW, _off[0]
            return o

        def f32v(o, words):
            return arena[:, o:o + words]

        def bfv(o, words):
            return arena[:, o:o + words].bitcast(BF16)

        mksem = lambda name: Sem(es.enter_context(nc.semaphore(name)))
        PE = Eng('PE', mksem('s_pe'))
        ACT = Eng('ACT', mksem('s_act'))
        DVE = Eng('DVE', mksem('s_dve'))
        POOL = Eng('POOL', mksem('s_pool'))
        SP = Eng('SP', mksem('s_sp'))
        engines = [PE, ACT, DVE, POOL, SP]
        all_sems = [e.sem for e in engines]

        def dsem(name):
            s = mksem(name)
            all_sems.append(s)
            return s

        o_par = alloc(P_N)
        par = f32v(o_par, P_N)
        parR = Res()
        par_sem = dsem('s_par')
        ident = par[:, P_ID:P_ID + 128]
        o_st = alloc(848)
        st_all = f32v(o_st, 848)
        stA = lambda l, j: st_all[:, (l * 8 + j) * 8:(l * 8 + j) * 8 + 8]
        stB = lambda l, j: st_all[:, 128 + (l * 8 + j) * 30:128 + (l * 8 + j) * 30 + 30]
        stC = lambda l, j: st_all[:, 608 + (l * 8 + j) * 15:608 + (l * 8 + j) * 15 + 15]
        stAR = [[Res() for _ in range(8)] for _ in range(L)]
        stBR = [[Res() for _ in range(8)] for _ in range(L)]
        stCR = [[Res() for _ in range(8)] for _ in range(L)]
        o_ones = alloc(128)
        onesD = bfv(o_ones, 64)
        onesC = bfv(o_ones + 64, 64)
        o_eps = alloc(2)
        epsR_ap = f32v(o_eps, 1)
        epsL_ap = f32v(o_eps + 1, 1)
        constR = Res()
        o_main = _off[0]

        o_x = alloc(16 * 512)
        xT = f32v(o_x, 16 * 512).rearrange("p (c t) -> p c t", t=512)
        xTR = [Res() for _ in range(16)]
        o_h = alloc(16 * 256)
        hT = bfv(o_h, 16 * 256).rearrange("p (c t) -> p c t", t=512)
        hTR = [Res() for _ in range(16)]
        o_y = alloc(24 * 256)
        yin = bfv(o_y, 24 * 256).rearrange("p (c t) -> p c t", t=512)
        yinR = [Res() for _ in range(24)]
        oT = f32v(o_h, 16 * 512).rearrange("p (c t) -> p c t", t=512)

        def oTR(c):
            return [hTR[2 * c], hTR[2 * c + 1]] if c < 8 else [yinR[2 * (c - 8)], yinR[2 * (c - 8) + 1]]
        o_big = alloc(16 * 256)
        vc = f32v(o_big, 8 * 512).rearrange("p (c t) -> p c t", t=512)
        mg = bfv(o_big, 16 * 256).rearrange("p (c t) -> p c t", t=512)
        bigR = [Res() for _ in range(16)]
        o_ring = alloc(NSLOT * CH * 64)
        ring = [bfv(o_ring + s * CH * 64, CH * 64) for s in range(NSLOT)]
        ringR = [Res() for _ in range(NSLOT)]
        ringsem = [dsem('s_ring%d' % s) for s in range(NSLOT)]
        o_tmp = alloc(NTMP * TMPW)
        tmp = Pool_([T(f32v(o_tmp + i * TMPW, TMPW)) for i in range(NTMP)])
        o_bf = alloc(NBF * 256)
        bfp = Pool_([T(bfv(o_bf + i * 256, 256)) for i in range(NBF)])
        o_gp = alloc(NGP * 512)
        gpool = Pool_([T(f32v(o_gp + i * 512, 512)) for i in range(NGP)])
        o_xs = alloc(2 * 2048)
        xs = [f32v(o_xs + s * 2048, 2048) for s in range(2)]
        xsR = [Res() for _ in range(2)]
        xs_ld = [dsem('s_xld%d' % s) for s in range(2)]
        xs_st = [dsem('s_xst%d' % s) for s in range(2)]

        fst = [f32v(o_main + s * 8192, 8192).rearrange("p (k c) -> p k c", c=512) for s in range(2)]
        bst = [bfv(o_main + 16384 + s * 4096, 4096).rearrange("p (b k c) -> p b k c", b=4, k=16) for s in range(2)]
        assert o_main + 16384 + 8192 <= AW
        fstR = [Res() for _ in range(2)]
        bstR = [[Res() for _ in range(4)] for _ in range(2)]
        f_ld = [dsem('s_fld%d' % s) for s in range(2)]
        b_st = [[dsem('s_bst%d_%d' % (s, b)) for b in range(4)] for s in range(2)]

        bankR = [Res() for _ in range(8)]
        busy = [False] * 8
        bptr = [0]

        def alloc_bank():
            for k in range(8):
                b = (bptr[0] + k) % 8
                if not busy[b]:
                    busy[b] = True
                    bptr[0] = (b + 1) % 8
                    return b
            raise AssertionError("no free PSUM bank")

        def release(b):
            assert busy[b]
            busy[b] = False

        def barrier():
            for E in engines:
                for s in all_sems:
                    if s.val > 0 and E.waited.get(s, 0) < s.val:
                        E.waited[s] = s.val
                        E.prog.append(('w', s, s.val))

        dma(SP, par, par_d[:, :], par_sem, writes=[parR])
        op(DVE, lambda h: h.memset(onesD, 1.0 / 2048.0), writes=[constR])
        op(DVE, lambda h: h.memset(onesC, 1.0 / 1024.0), writes=[constR])
        op(DVE, lambda h: h.memset(epsR_ap, RMS_EPS), writes=[constR])
        op(DVE, lambda h: h.memset(epsL_ap, LN_EPS), writes=[constR])
        allst = [r for l in range(L) for j in range(8) for r in (stAR[l][j], stBR[l][j], stCR[l][j])]
        op(DVE, lambda h: h.memset(st_all, 0.0), writes=allst)

        unit_i = [0]

        def conv_unit(l, tname, src2d, K, ncol, c0):
            s = unit_i[0] % 2
            unit_i[0] += 1
            ncb = ncol // 128
            src = src2d.rearrange("(k p) c -> p k c", p=128)[:, :, c0:c0 + ncol]
            dma(SP, fst[s][:, 0:K, 0:ncol], src, f_ld[s], writes=[fstR[s]])
            for cb in range(ncb):
                E = ACT if cb % 2 == 0 else DVE
                o_ap = bst[s][:, cb, 0:K, :]
                i_ap = fst[s][:, 0:K, cb * 128:(cb + 1) * 128]
                if E is ACT:
                    op(E, lambda h, o=o_ap, i=i_ap: h.activation(out=o, in_=i, func=AF.Copy),
                       reads=[fstR[s]], writes=[bstR[s][cb]])
                else:
                    op(E, lambda h, o=o_ap, i=i_ap: h.tensor_copy(out=o, in_=i),
                       reads=[fstR[s]], writes=[bstR[s][cb]])
                p0 = jobpos[(tname, c0 // 128 + cb)]
                dst = wsc[l][:, p0 * 128:(p0 + K) * 128].rearrange("p (k c) -> p k c", c=128)
                dma(POOL, dst, o_ap, b_st[s][cb], reads=[bstR[s][cb]])

        for l in range(L):
            for u in range(IN_COLS // 512):
                conv_unit(l, 'w_in', w_in_d[l], 16, 512, u * 512)
            for g in range(4):
                conv_unit(l, 'gw%d' % g, gw_d[l, g], 2, 256, 0)
            for b in 'abc':
                for u in range(4):
                    conv_unit(l, 'w_br_' + b, w_br_d[b][l], 8, 512, u * 512)
            for u in range(4):
                conv_unit(l, 'w_o', w_o_d[l], 16, 512, u * 512)
        barrier()

        STATE_KINDS = ('au', 'ac', 'bg', 'bv', 'cu')
        used = set()
        for (kind_, idx_, src_) in jobs:
            if src_ is not None and kind_ in STATE_KINDS:
                p0_ = jobpos[(src_[0], src_[1])]
                for bb_ in range(p0_, p0_ + src_[2]):
                    used.add(bb_ // CH)
        chunk_list = []
        for gl_ in range(len(tiles) * L):
            for c_ in range(NCHUNK):
                if gl_ == L - 1 and c_ not in used:
                    continue
                chunk_list.append((gl_, c_))
        chunk_seq = {gc: i for i, gc in enumerate(chunk_list)}
        total_chunks = len(chunk_list)
        issued = [0]

        def issue_chunk(seq):
            gl, c = chunk_list[seq]
            l = gl % L
            slot = seq % NSLOT
            dma(SP, ring[slot], wsc[l][:, c * CH * 128:(c + 1) * CH * 128], ringsem[slot], writes=[ringR[slot]])

        def wblock(gl, b):
            seq = chunk_seq[(gl, b // CH)]
            lim = min(seq + NSLOT - 1, total_chunks - 1)
            while issued[0] <= lim:
                issue_chunk(issued[0])
                issued[0] += 1
            slot = seq % NSLOT
            return ring[slot][:, (b % CH) * 128:(b % CH) * 128 + 128], ringR[slot]

        def PSB(b, n):
            return ps[:, b, 0:n]

        pc = lambda i: par[:, i:i + 1]

        stg = [(f32v(o_y + 16 * 256, 2048), yinR[16:24], dsem('s_pf0')),
               (f32v(o_big, 2048), bigR[0:8], dsem('s_pf1')),
               (f32v(o_big + 2048, 2048), bigR[8:16], dsem('s_pf2')),
               (xs[0], [xsR[0]], xs_ld[0])]
        pre_issued = set()

        def issue_x(ti, sub):
            if (ti, sub) in pre_issued or ti >= len(tiles):
                return
            pre_issued.add((ti, sub))
            t0, n = tiles[ti]
            ap, rl, sem = stg[sub]
            dma(POOL, ap[:, :], xh[t0 + sub * 128:t0 + sub * 128 + 128, :], sem, writes=rl)

        def load_x(ti, t0, n):
            nsub = max(1, n // 128)
            for sub in range(nsub):
                nt = min(128, n)
                if n < 128:
                    src_ap, src_r = xs[0], [xsR[0]]
                    dma(POOL, xs[0][0:nt, :], xh[t0:t0 + nt, :], xs_ld[0], writes=[xsR[0]])
                else:
                    issue_x(ti, sub)
                    src_ap, src_r = stg[sub][0], stg[sub][1]
                for q4 in range(4):
                    b = alloc_bank()
                    for q in range(4):
                        c = q4 * 4 + q
                        op(PE, lambda h, o=ps[:, b, q * 128:q * 128 + nt], i=src_ap[0:nt, c * 128:(c + 1) * 128], idn=ident[0:nt, 0:nt]:
                           h.transpose(out=o, in_=i, identity=idn),
                           reads=src_r + [parR], writes=[bankR[b]])
                    o_ap = xT[:, q4 * 4:q4 * 4 + 4, sub * 128:sub * 128 + nt]
                    i_ap = ps[:, b, :].rearrange("p (q t) -> p q t", t=128)[:, :, 0:nt]
                    wr = [xTR[q4 * 4 + q] for q in range(4)]
                    if q4 % 2 == 0:
                        op(ACT, lambda h, o=o_ap, i=i_ap: h.activation(out=o, in_=i, func=AF.Copy),
                           reads=[bankR[b]], writes=wr)
                    else:
                        op(DVE, lambda h, o=o_ap, i=i_ap: h.tensor_copy(out=o, in_=i),
                           reads=[bankR[b]], writes=wr)
                    release(b)

        def rms_stat(b, c, n, sq):
            op(PE, lambda h, o=PSB(b, n), r=sq.ap[:, 0:n], c=c: h.matmul(o, onesD, r, start=(c == 0), stop=(c == 15)),
               reads=[sq.res, constR], writes=[bankR[b]])
            bfp.free(sq)

        def rms_sq(c, n):
            sq = bfp.get()
            op(ACT, lambda h, o=sq.ap[:, 0:n], i=xT[:, c, 0:n]: h.activation(out=o, in_=i, func=AF.Square),
               reads=[xTR[c]], writes=[sq.res])
            return sq

        def rms(gcol, n, final, b=None):
            if b is None:
                b = alloc_bank()
                for c in range(16):
                    rms_stat(b, c, n, rms_sq(c, n))
            sd = tmp.get()
            op(ACT, lambda h, o=sd.ap[:, 0:n], i=PSB(b, n): h.activation(out=o, in_=i, func=AF.Sqrt, bias=epsR_ap, scale=1.0),
               reads=[bankR[b], constR], writes=[sd.res])
            release(b)
            rs = tmp.get()
            op(DVE, lambda h, o=rs.ap[:, 0:n], i=sd.ap[:, 0:n]: h.reciprocal(out=o, in_=i), reads=[sd.res], writes=[rs.res])
            tmp.free(sd)
            for c in range(16):
                if final:
                    o_ap, wr = oT[:, c, 0:n], oTR(c)
                else:
                    o_ap, wr = hT[:, c, 0:n], [hTR[c]]
                op(DVE, lambda h, o=o_ap, i=xT[:, c, 0:n], sc=pc(gcol + c), r=rs.ap[:, 0:n]:
                   h.scalar_tensor_tensor(out=o, in0=i, scalar=sc, in1=r, op0=ALU.mult, op1=ALU.mult),
                   reads=[xTR[c], rs.res, parR], writes=wr)
            tmp.free(rs)

        def store_out(orow0, n, rc=None):
            for sub in range(n // 128):
                s = sub % 2
                for q4 in range(4):
                    b = alloc_bank()
                    for q in range(4):
                        c = q4 * 4 + q
                        op(PE, lambda h, o=ps[:, b, q * 128:(q + 1) * 128], i=oT[:, c, sub * 128:(sub + 1) * 128]:
                           h.transpose(out=o, in_=i, identity=ident),
                           reads=oTR(c) + [parR], writes=[bankR[b]])
                    o_ap = xs[s][:, q4 * 512:(q4 + 1) * 512]
                    sc = rc.ap[:, sub:sub + 1]
                    if q4 % 2 == 0:
                        op(ACT, lambda h, o=o_ap, i=ps[:, b, :], sc=sc: h.activation(out=o, in_=i, func=AF.Copy, scale=sc),
                           reads=[bankR[b], rc.res], writes=[xsR[s]])
                    else:
                        op(DVE, lambda h, o=o_ap, i=ps[:, b, :], sc=sc: h.tensor_scalar(out=o, in0=i, scalar1=sc, scalar2=None, op0=ALU.mult),
                           reads=[bankR[b], rc.res], writes=[xsR[s]])
                    release(b)
                dma(POOL, out_d[orow0 + sub * 128:orow0 + (sub + 1) * 128, :], xs[s][:, :], xs_st[s], reads=[xsR[s]])

        def final_out(rb, orow0, n):
            row = tmp.get()
            op(ACT, lambda h, o=row.ap[0:1, 0:n], i=ps[0:1, rb, 0:n]: h.activation(out=o, in_=i, func=AF.Copy),
               reads=[bankR[rb]], writes=[row.res])
            release(rb)
            bc = alloc_bank()
            for sub in range(n // 128):
                op(PE, lambda h, o=ps[:, bc, sub:sub + 1], i=row.ap[0:1, sub * 128:(sub + 1) * 128]:
                   h.transpose(out=o, in_=i, identity=ident[0:1, 0:1]),
                   reads=[row.res, parR], writes=[bankR[bc]])
            nsub = n // 128
            sd = tmp.get()
            op(ACT, lambda h, o=sd.ap[:, 0:nsub], i=ps[:, bc, 0:nsub]: h.activation(out=o, in_=i, func=AF.Sqrt, bias=epsR_ap, scale=1.0),
               reads=[bankR[bc], constR], writes=[sd.res])
            release(bc)
            tmp.free(row)
            rc = tmp.get()
            op(DVE, lambda h, o=rc.ap[:, 0:nsub], i=sd.ap[:, 0:nsub]: h.reciprocal(out=o, in_=i), reads=[sd.res], writes=[rc.res])
            tmp.free(sd)
            store_out(orow0, n, rc)
            tmp.free(rc)

        def run_layer(ti, l, n, state_only=False):
            gl = ti * L + l
            S = {}

            def rhs_for(kind, idx, kc):
                if kind == 'ya':
                    return yin[:, kc, 0:n], yinR[kc]
                if kind == 'yb':
                    return yin[:, 8 + kc, 0:n], yinR[8 + kc]
                if kind == 'yc':
                    return yin[:, 16 + kc, 0:n], yinR[16 + kc]
                if kind == 'o':
                    return mg[:, kc, 0:n], bigR[kc]
                if kind == 'gy':
                    d = S['d', (idx // 2) * 2 + kc]
                    return d.ap[:, 0:n], d.res
                return hT[:, kc, 0:n], hTR[kc]

            def wjob(kind, idx, src):
                b = alloc_bank()
                tname, cb, K = src
                p0 = jobpos[(tname, cb)]
                for kc in range(K):
                    wap, wres = wblock(gl, p0 + kc)
                    r_ap, r_res = rhs_for(kind, idx, kc)
                    op(PE, lambda h, o=PSB(b, n), w=wap, r=r_ap, kc=kc, K=K: h.matmul(o, w, r, start=(kc == 0), stop=(kc == K - 1)),
                       reads=[wres, r_res], writes=[bankR[b]])
                return b

            pend = []

            def drain(k):
                for _ in range(min(k, len(pend))):
                    pend.pop(0)[1]()

            def drain_upto(jmax):
                while pend and pend[0][0] <= jmax:
                    pend.pop(0)[1]()

            def act(fn, reads, writes):
                op(ACT, fn, reads, writes)

            def dve(fn, reads, writes, **kw):
                op(DVE, fn, reads, writes, **kw)

            def bz_stage1(j):
                rstd, nmr = S['rstd'], S['nmr']
                vres = [bigR[2 * j], bigR[2 * j + 1]]
                c1 = tmp.get()
                dve(lambda h, o=c1.ap[:, 0:n], a=vc[:, j, 0:n], r=rstd.ap[:, 0:n]: h.tensor_tensor(out=o, in0=a, in1=r, op=ALU.mult),
                    vres + [rstd.res], [c1.res])
                dve(lambda h, o=c1.ap[:, 0:n], r=nmr.ap[:, 0:n]: h.tensor_tensor(out=o, in0=o, in1=r, op=ALU.add),
                    [c1.res, nmr.res], [c1.res])
                act(lambda h, o=c1.ap[:, 0:n], g=pc(P_LG + l * 8 + j), bb=pc(P_LB + l * 8 + j):
                    h.activation(out=o, in_=o, func=AF.Silu, bias=bb, scale=g),
                    [c1.res, parR], [c1.res])
                S['c1', j] = c1

            def bz_consume(j, b):
                c1 = S.pop(('c1', j))
                sz = tmp.get()
                act(lambda h, o=sz.ap[:, 0:n], i=PSB(b, n): h.activation(out=o, in_=i, func=AF.Silu), [bankR[b]], [sz.res])
                release(b)
                dve(lambda h, o=yin[:, 8 + j, 0:n], a=c1.ap[:, 0:n], s=sz.ap[:, 0:n]: h.tensor_tensor(out=o, in0=a, in1=s, op=ALU.mult),
                    [c1.res, sz.res], [yinR[8 + j]])
                tmp.free(c1, sz)
                if j + BZ_AHEAD < 8:
                    bz_stage1(j + BZ_AHEAD)
                if j == 7:
                    tmp.free(S.pop('rstd'), S.pop('nmr'))

            for (kind, idx, src) in jobs:
                j = idx
                if state_only:
                    if kind not in STATE_KINDS:
                        continue
                    b = wjob(kind, idx, src)
                    if kind == 'au' or kind == 'bg':
                        t_ = tmp.get()
                        fn_ = AF.Copy if kind == 'au' else AF.Sigmoid
                        act(lambda h, o=t_.ap[:, 0:n], i=PSB(b, n), fn_=fn_: h.activation(out=o, in_=i, func=fn_), [bankR[b]], [t_.res])
                        release(b)
                        S['so'] = t_
                    elif kind == 'ac' or kind == 'bv':
                        t_ = S.pop('so')
                        H_, st_, stR_ = (8, stA, stAR) if kind == 'ac' else (30, stB, stBR)
                        mb = tmp.get()
                        dve(lambda h, o=mb.ap[:, H_:H_ + n], a=PSB(b, n), u=t_.ap[:, 0:n]: h.tensor_tensor(out=o, in0=a, in1=u, op=ALU.mult),
                            [bankR[b], t_.res], [mb.res])
                        release(b)
                        tmp.free(t_)
                        op(DVE, lambda h, o=st_(l, j), i=mb.ap[:, n:n + H_]: h.tensor_copy(out=o, in_=i), [mb.res], [stR_[l][j]])
                        tmp.free(mb)
                    else:
                        ub = tmp.get()
                        act(lambda h, o=ub.ap[:, 15:15 + n], i=PSB(b, n): h.activation(out=o, in_=i, func=AF.Copy), [bankR[b]], [ub.res])
                        release(b)
                        op(DVE, lambda h, o=stC(l, j), i=ub.ap[:, n:n + 15]: h.tensor_copy(out=o, in_=i), [ub.res], [stCR[l][j]])
                        tmp.free(ub)
                    continue
                if kind == 'st':
                    drain_upto(j)
                    vcb, sqb = bfp.get(), bfp.get()
                    vres_ = [bigR[2 * j], bigR[2 * j + 1]]
                    act(lambda h, o=vcb.ap[:, 0:n], i=vc[:, j, 0:n]: h.activation(out=o, in_=i, func=AF.Copy), vres_, [vcb.res])
                    act(lambda h, o=sqb.ap[:, 0:n], i=vc[:, j, 0:n]: h.activation(out=o, in_=i, func=AF.Square), vres_, [sqb.res])
                    if j == 0:
                        S['bm'], S['be'] = alloc_bank(), alloc_bank()
                    bm, be = S['bm'], S['be']
                    op(PE, lambda h, o=PSB(bm, n), r=vcb.ap[:, 0:n], j=j: h.matmul(o, onesC, r, start=(j == 0), stop=(j == 7)),
                       reads=[vcb.res, constR], writes=[bankR[bm]])
                    op(PE, lambda h, o=PSB(be, n), r=sqb.ap[:, 0:n], j=j: h.matmul(o, onesC, r, start=(j == 0), stop=(j == 7)),
                       reads=[sqb.res, constR], writes=[bankR[be]])
                    bfp.free(vcb, sqb)
                    if j == 7:
                        msb, m2, rstd = tmp.get(), tmp.get(), tmp.get()
                        act(lambda h, o=msb.ap[:, 0:n], i=PSB(bm, n): h.activation(out=o, in_=i, func=AF.Copy), [bankR[bm]], [msb.res])
                        dve(lambda h, o=m2.ap[:, 0:n], a=msb.ap[:, 0:n]: h.tensor_tensor(out=o, in0=a, in1=a, op=ALU.mult), [msb.res], [m2.res])
                        dve(lambda h, o=m2.ap[:, 0:n], a=PSB(be, n): h.tensor_tensor(out=o, in0=a, in1=o, op=ALU.subtract),
                            [bankR[be], m2.res], [m2.res])
                        release(bm)
                        release(be)
                        S.pop('bm')
                        S.pop('be')
                        act(lambda h, o=m2.ap[:, 0:n]: h.activation(out=o, in_=o, func=AF.Sqrt, bias=epsL_ap, scale=1.0),
                            [m2.res, constR], [m2.res])
                        dve(lambda h, o=rstd.ap[:, 0:n], i=m2.ap[:, 0:n]: h.reciprocal(out=o, in_=i), [m2.res], [rstd.res])
                        dve(lambda h, o=msb.ap[:, 0:n], r=rstd.ap[:, 0:n]:
                            h.scalar_tensor_tensor(out=o, in0=o, scalar=-1.0, in1=r, op0=ALU.mult, op1=ALU.mult),
                            [msb.res, rstd.res], [msb.res])
                        tmp.free(m2)
                        S['rstd'], S['nmr'] = rstd, msb
                        for jj in range(BZ_AHEAD):
                            bz_stage1(jj)
                        for (jj, bb) in S.pop('bz_pending'):
                            bz_consume(jj, bb)
                    continue

                b = wjob(kind, idx, src)
                drain(DRAIN_K)

                if kind == 'au':
                    au = tmp.get()
                    act(lambda h, o=au.ap[:, 0:n], i=PSB(b, n): h.activation(out=o, in_=i, func=AF.Copy), [bankR[b]], [au.res])
                    release(b)
                    S['au'] = au
                elif kind == 'ac':
                    au = S.pop('au')
                    mb = tmp.get()
                    op(DVE, lambda h, o=mb.ap[:, 0:8], i=stA(l, j): h.tensor_copy(out=o, in_=i), [stAR[l][j]], [mb.hres])
                    dve(lambda h, o=mb.ap[:, 8:8 + n], a=PSB(b, n), u=au.ap[:, 0:n]: h.tensor_tensor(out=o, in0=a, in1=u, op=ALU.mult),
                        [bankR[b], au.res], [mb.res])
                    release(b)
                    tmp.free(au)
                    op(DVE, lambda h, o=stA(l, j), i=mb.ap[:, n:n + 8]: h.tensor_copy(out=o, in_=i), [mb.res], [stAR[l][j]])
                    cv = tmp.get()
                    wc = P_AW + (l * 8 + j) * 3
                    dve(lambda h, o=cv.ap[:, 0:n], i=mb.ap[:, 6:6 + n], w=pc(wc):
                        h.tensor_scalar(out=o, in0=i, scalar1=w, scalar2=None, op0=ALU.mult),
                        [mb.res, mb.hres, parR], [cv.res])
                    for k in (1, 2):
                        dve(lambda h, o=cv.ap[:, 0:n], i=mb.ap[:, 6 + k:6 + k + n], w=pc(wc + k):
                            h.scalar_tensor_tensor(out=o, in0=i, scalar=w, in1=o, op0=ALU.mult, op1=ALU.add),
                            [mb.res, mb.hres, cv.res, parR], [cv.res])
                    tmp.free(mb)
                    S['cv'] = cv
                elif kind == 'ab':
                    cv = S.pop('cv')
                    y1 = tmp.get()
                    dve(lambda h, o=y1.ap[:, 0:n], a=PSB(b, n), c=cv.ap[:, 0:n]: h.tensor_tensor(out=o, in0=a, in1=c, op=ALU.mult),
                        [bankR[b], cv.res], [y1.res])
                    release(b)
                    tmp.free(cv)
                    S['y1'] = y1
                elif kind == 'az':
                    y1 = S.pop('y1')
                    sz = tmp.get()
                    act(lambda h, o=sz.ap[:, 0:n], i=PSB(b, n): h.activation(out=o, in_=i, func=AF.Silu), [bankR[b]], [sz.res])
                    release(b)
                    dve(lambda h, o=yin[:, j, 0:n], a=y1.ap[:, 0:n], s=sz.ap[:, 0:n]: h.tensor_tensor(out=o, in0=a, in1=s, op=ALU.mult),
                        [y1.res, sz.res], [yinR[j]])
                    tmp.free(y1, sz)
                elif kind == 'bg':
                    sg = tmp.get()
                    act(lambda h, o=sg.ap[:, 0:n], i=PSB(b, n): h.activation(out=o, in_=i, func=AF.Sigmoid), [bankR[b]], [sg.res])
                    release(b)
                    S['sg'] = sg
                elif kind == 'bv':
                    sg = S.pop('sg')
                    vb = tmp.get()
                    op(DVE, lambda h, o=vb.ap[:, 0:30], i=stB(l, j): h.tensor_copy(out=o, in_=i), [stBR[l][j]], [vb.hres])
                    dve(lambda h, o=vb.ap[:, 30:30 + n], a=PSB(b, n), s=sg.ap[:, 0:n]: h.tensor_tensor(out=o, in0=a, in1=s, op=ALU.mult),
                        [bankR[b], sg.res], [vb.res])
                    release(b)
                    tmp.free(sg)
                    op(DVE, lambda h, o=stB(l, j), i=vb.ap[:, n:n + 30]: h.tensor_copy(out=o, in_=i), [vb.res], [stBR[l][j]])
                    vres = [bigR[2 * j], bigR[2 * j + 1]]
                    wc = P_BW + (l * 8 + j) * 31
                    dve(lambda h, o=vc[:, j, 0:n], i=vb.ap[:, 0:n], w=pc(wc), bb=pc(P_BB + l * 8 + j):
                        h.tensor_scalar(out=o, in0=i, scalar1=w, scalar2=bb, op0=ALU.mult, op1=ALU.add),
                        [vb.res, vb.hres, parR], vres)
                    a1 = tmp.get()
                    pend.append((j, lambda vb=vb, wc=wc, a1=a1: dve(
                        lambda h, o=a1.ap[:, 0:n], i=vb.ap[:, 1:1 + n], w=pc(wc + 1):
                        h.tensor_scalar(out=o, in0=i, scalar1=w, scalar2=None, op0=ALU.mult),
                        [vb.res, vb.hres, parR], [a1.res], big=True)))
                    for k in range(2, 31):
                        if k % 2 == 0:
                            pend.append((j, lambda j=j, k=k, vb=vb, wc=wc, vres=vres: dve(
                                lambda h, o=vc[:, j, 0:n], i=vb.ap[:, k:k + n], w=pc(wc + k):
                                h.scalar_tensor_tensor(out=o, in0=i, scalar=w, in1=o, op0=ALU.mult, op1=ALU.add),
                                [vb.res, vb.hres, parR] + vres, vres, relax=True, big=True)))
                        else:
                            pend.append((j, lambda k=k, vb=vb, wc=wc, a1=a1: dve(
                                lambda h, o=a1.ap[:, 0:n], i=vb.ap[:, k:k + n], w=pc(wc + k):
                                h.scalar_tensor_tensor(out=o, in0=i, scalar=w, in1=o, op0=ALU.mult, op1=ALU.add),
                                [vb.res, vb.hres, parR, a1.res], [a1.res], relax=True, big=True)))
                    pend.append((j, lambda j=j, a1=a1, vres=vres: dve(
                        lambda h, o=vc[:, j, 0:n], a=a1.ap[:, 0:n]: h.tensor_tensor(out=o, in0=o, in1=a, op=ALU.add),
                        [a1.res] + vres, vres)))
                    pend.append((j, lambda vb=vb, a1=a1: tmp.free(vb, a1)))
                elif kind == 'bz':
                    if 'rstd' not in S:
                        S.setdefault('bz_pending', []).append((j, b))
                    else:
                        bz_consume(j, b)
                elif kind == 'cu':
                    g = j // 2
                    w = 2 << g
                    ub = tmp.get()
                    op(DVE, lambda h, o=ub.ap[:, 0:15], i=stC(l, j): h.tensor_copy(out=o, in_=i), [stCR[l][j]], [ub.hres])
                    act(lambda h, o=ub.ap[:, 15:15 + n], i=PSB(b, n): h.activation(out=o, in_=i, func=AF.Copy), [bankR[b]], [ub.res])
                    release(b)
                    op(DVE, lambda h, o=stC(l, j), i=ub.ap[:, n:n + 15]: h.tensor_copy(out=o, in_=i), [ub.res], [stCR[l][j]])
                    c0 = 15 - (w - 2)
                    ln = n + w - 2
                    cur = tmp.get()
                    dve(lambda h, o=cur.ap[:, 0:ln], a=ub.ap[:, c0:c0 + ln], bb=ub.ap[:, c0 - 1:c0 - 1 + ln]:
                        h.tensor_tensor(out=o, in0=a, in1=bb, op=ALU.add), [ub.res, ub.hres], [cur.res])
                    step = 2
                    while step < w:
                        ln -= step
                        nxt = tmp.get()
                        dve(lambda h, o=nxt.ap[:, 0:ln], a=cur.ap[:, step:step + ln], bb=cur.ap[:, 0:ln]:
                            h.tensor_tensor(out=o, in0=a, in1=bb, op=ALU.add), [cur.res], [nxt.res])
                        tmp.free(cur)
                        cur = nxt
                        step *= 2
                    assert ln == n
                    d = bfp.get()
                    dve(lambda h, o=d.ap[:, 0:n], s=cur.ap[:, 0:n], u=ub.ap[:, 15:15 + n], iw=1.0 / w:
                        h.scalar_tensor_tensor(out=o, in0=s, scalar=iw, in1=u, op0=ALU.mult, op1=ALU.subtract),
                        [cur.res, ub.res], [d.res])
                    if ti == 1:
                        fx = tmp.get()
                        dve(lambda h, o=fx.ap[:, 0:16], s=cur.ap[:, 0:16], ic=par[:, P_IC + g * 16:P_IC + g * 16 + 16]:
                            h.tensor_tensor(out=o, in0=s, in1=ic, op=ALU.mult), [cur.res, parR], [fx.res])
                        dve(lambda h, o=d.ap[:, 0:16], a=fx.ap[:, 0:16], u=ub.ap[:, 15:31]:
                            h.tensor_tensor(out=o, in0=a, in1=u, op=ALU.subtract), [fx.res, ub.res, d.res], [d.res])
                        tmp.free(fx)
                    tmp.free(cur, ub)
                    S['d', j] = d
                elif kind == 'cz':
                    szc = tmp.get()
                    act(lambda h, o=szc.ap[:, 0:n], i=PSB(b, n): h.activation(out=o, in_=i, func=AF.Silu), [bankR[b]], [szc.res])
                    release(b)
                    S['szc', j] = szc
                elif kind == 'gy':
                    szc = S.pop(('szc', j))
                    dve(lambda h, o=yin[:, 16 + j, 0:n], a=PSB(b, n), sc=pc(P_CS + l * 8 + j), s=szc.ap[:, 0:n]:
                        h.scalar_tensor_tensor(out=o, in0=a, scalar=sc, in1=s, op0=ALU.mult, op1=ALU.mult),
                        [bankR[b], szc.res, parR], [yinR[16 + j]])
                    release(b)
                    tmp.free(szc)
                    if j % 2 == 1:
                        bfp.free(S.pop(('d', j - 1)), S.pop(('d', j)))
                elif kind in ('g0', 'g1', 'g2'):
                    pool_ = gpool if j in HOIST else tmp
                    sg = pool_.get()
                    act(lambda h, o=sg.ap[:, 0:n], i=PSB(b, n): h.activation(out=o, in_=i, func=AF.Sigmoid), [bankR[b]], [sg.res])
                    release(b)
                    S[kind, j] = (sg, pool_)
                elif kind == 'ya':
                    sg, pool_ = S.pop(('g0', j))
                    acc = tmp.get()
                    dve(lambda h, o=acc.ap[:, 0:n], a=PSB(b, n), s=sg.ap[:, 0:n]: h.tensor_tensor(out=o, in0=a, in1=s, op=ALU.mult),
                        [bankR[b], sg.res], [acc.res])
                    release(b)
                    pool_.free(sg)
                    S['acc'] = acc
                elif kind in ('yb', 'yc'):
                    sg, pool_ = S.pop(('g1' if kind == 'yb' else 'g2', j))
                    acc = S['acc']
                    p = tmp.get()
                    dve(lambda h, o=p.ap[:, 0:n], a=PSB(b, n), s=sg.ap[:, 0:n]: h.tensor_tensor(out=o, in0=a, in1=s, op=ALU.mult),
                        [bankR[b], sg.res], [p.res])
                    release(b)
                    pool_.free(sg)
                    if kind == 'yb':
                        dve(lambda h, o=acc.ap[:, 0:n], pp=p.ap[:, 0:n]: h.tensor_tensor(out=o, in0=o, in1=pp, op=ALU.add),
                            [acc.res, p.res], [acc.res])
                        tmp.free(p)
                    else:
                        dve(lambda h, o=mg[:, j, 0:n], a=acc.ap[:, 0:n], pp=p.ap[:, 0:n]: h.tensor_tensor(out=o, in0=a, in1=pp, op=ALU.add),
                            [acc.res, p.res], [bigR[j]])
                        tmp.free(p, acc)
                        S.pop('acc')
                elif kind == 'o':
                    if j == 0:
                        S['rb'] = alloc_bank()
                        S['rq'] = []
                        if l == L - 1:
                            issue_x(ti + 1, 0)
                    dve(lambda h, o=xT[:, j, 0:n], a=PSB(b, n): h.tensor_tensor(out=o, in0=a, in1=o, op=ALU.add),
                        [bankR[b], xTR[j]], [xTR[j]])
                    release(b)
                    S['rq'].append((j, rms_sq(j, n)))
                    if l == L - 1 and ti > 0:
                        act(lambda h, o=oT[:, j, 0:n], i=xT[:, j, 0:n], g=pc(P_FG + j): h.activation(out=o, in_=i, func=AF.Copy, scale=g),
                            [xTR[j], parR], oTR(j))
                    while S['rq'] and S['rq'][0][0] <= j - 2:
                        c, sq = S['rq'].pop(0)
                        rms_stat(S['rb'], c, n, sq)
                else:
                    raise AssertionError(kind)
            if state_only:
                assert not S and not pend
                return None
            for c, sq in S.pop('rq'):
                rms_stat(S['rb'], c, n, sq)
            rb = S.pop('rb')
            if l == L - 1:
                issue_x(ti + 1, 1)
                issue_x(ti + 1, 2)
            assert not S, list(S.keys())
            assert not pend
            return rb

        for ti, (t0, n) in enumerate(tiles):
            load_x(ti, t0, n)
            rb = None
            for l in range(L):
                rms(P_NG + l * 16, n, final=False, b=rb)
                rb = run_layer(ti, l, n, state_only=(ti == 0 and l == L - 1))
            if ti > 0:
                final_out(rb, t0 - HALO, n)
        for s in range(2):
            POOL.prog.append(('w', xs_st[s], xs_st[s].val))
        assert issued[0] == total_chunks

        block = es.enter_context(nc.Block())

        def runner(E):
            def body(h):
                for it in E.prog:
                    if it[0] == 'w':
                        h.wait_ge(it[1].h, it[2])
                    elif it[0] == 'i':
                        it[1](h).then_inc(E.sem.h, 1)
                    else:
                        h.dma_start(out=it[1], in_=it[2]).then_inc(it[3].h, 16)
            return body
        block.sync(runner(SP))
        block.tensor(runner(PE))
        block.scalar(runner(ACT))
        block.vector(runner(DVE))
        block.gpsimd(runner(POOL))
    return nc


def pack_params(core, seq_start, norm_g, final_g, a_conv_w, b_conv_w, b_conv_b, b_ln_g, b_ln_b, c_scale):
    p = np.zeros((128, P_N), np.float32)
    p[:, P_NG:P_NG + 32] = norm_g.reshape(2, 16, 128).transpose(2, 0, 1).reshape(128, 32)
    p[:, P_FG:P_FG + 16] = final_g.reshape(16, 128).T
    p[:, P_AW:P_AW + 48] = a_conv_w.reshape(2, 3, 8, 128).transpose(3, 0, 2, 1).reshape(128, 48)
    p[:, P_BW:P_BW + 496] = b_conv_w.reshape(2, 31, 8, 128).transpose(3, 0, 2, 1).reshape(128, 496)
    for off, v in ((P_BB, b_conv_b), (P_LG, b_ln_g), (P_LB, b_ln_b), (P_CS, c_scale)):
        p[:, off:off + 16] = v.reshape(2, 8, 128).transpose(2, 0, 1).reshape(128, 16)
    ic = np.zeros((4, 16), np.float32)
    for g in range(4):
        w = 2 << g
        for t in range(16):
            ic[g, t] = 1.0 / (min(t + 1, w) if seq_start else w)
    p[:, P_IC:P_IC + 64] = ic.reshape(1, 64)
    p[:, P_ID:P_ID + 128] = np.eye(128, dtype=np.float32)
    return p


_NC_CACHE = {}


def kernel(x, norm_g, w_in, a_conv_w, b_conv_w, b_conv_b, b_ln_g, b_ln_b,
           c_group_w, c_scale, w_br_a, w_br_b, w_br_c, w_o, final_g):
    x = np.asarray(x, np.float32)
    B, S, _ = x.shape
    per_seq = NCORES // B
    tpc = S // per_seq
    if tpc not in _NC_CACHE:
        _NC_CACHE[tpc] = build_program(tpc)
    nc = _NC_CACHE[tpc]
    f = lambda a: np.ascontiguousarray(np.asarray(a, np.float32))
    w_in, w_br_a, w_br_b, w_br_c, w_o, c_group_w = map(f, (w_in, w_br_a, w_br_b, w_br_c, w_o, c_group_w))
    smalls = [f(a) for a in (norm_g, final_g, a_conv_w, b_conv_w, b_conv_b, b_ln_g, b_ln_b, c_scale)]
    in_maps = []
    for core in range(NCORES):
        b, q = divmod(core, per_seq)
        s0 = q * tpc
        xhh = np.zeros((HALO + tpc, D), np.float32)
        if s0 == 0:
            xhh[HALO:] = x[b, 0:tpc]
        else:
            xhh[:] = x[b, s0 - HALO:s0 + tpc]
        in_maps.append({
            "xh": xhh, "par": pack_params(core, s0 == 0, *smalls),
            "w_in": w_in, "w_br_a": w_br_a, "w_br_b": w_br_b, "w_br_c": w_br_c,
            "w_o": w_o, "c_group_w": c_group_w,
        })
    res = run_bass_kernel_spmd(nc, in_maps, core_ids=list(range(NCORES)))
    out = np.empty((B, S, D), np.float32)
    for core in range(NCORES):
        b, q = divmod(core, per_seq)
        out[b, q * tpc:(q + 1) * tpc] = res.results[core]["out"]
    return out
```
